# Optimizing a Trainium2 kernel written in Bass

```python
import jax, jax.numpy as jnp
from jax import lax
import numpy as np

D_MODEL = 2048
BATCH = 8
SEQ = 2048
DEPTH = 2

GRID_W = 64
CTX_LEN = 256
N_MIXERS = 2
EXPAND = 2
D_INNER = EXPAND * D_MODEL
EPS = 1e-6
F32 = jnp.float32
A_HEADS = 16
A_DK = 128
A_DF = A_HEADS * A_DK
A_DV = D_INNER // A_HEADS
A_CHUNK = 32
A_IN_COLS = 3 * A_DF + 2 * D_INNER
B_HEADDIM = 64
B_HEADS = D_INNER // B_HEADDIM
B_GROUPS = 8
B_HPG = B_HEADS // B_GROUPS
B_STATE = 128
B_GS = B_GROUPS * B_STATE
B_CONV = 5
B_CHUNK = 128
B_XBC = D_INNER + 2 * B_GS
B_IN_COLS = D_INNER + B_XBC + 2 * B_HEADS

kernel_name = "bidir_hgrn2_mamba2_interleaved_prefix_ctx"


def rmsnorm(x, w):
    xf = x.astype(F32)
    y = xf * lax.rsqrt(jnp.mean(xf * xf, axis=-1, keepdims=True) + EPS)
    return (y * w.astype(F32)).astype(x.dtype)


def to_col_major(t, rows):
    b, l, ch = t.shape
    return t.reshape(b, rows, GRID_W, ch).transpose(0, 2, 1, 3).reshape(b, l, ch)


def from_col_major(t, rows):
    b, l, ch = t.shape
    return t.reshape(b, GRID_W, rows, ch).transpose(0, 2, 1, 3).reshape(b, l, ch)


def _bidir(t_fw, t_bw):
    return jnp.concatenate([t_fw, jnp.flip(t_bw, axis=1)], axis=0)


def _merge(t):
    b = t.shape[0] // 2
    return t[:b] + jnp.flip(t[b:], axis=1)


def _chunks(t, size):
    n, l = t.shape[:2]
    return jnp.moveaxis(t.reshape((n, l // size, size) + t.shape[2:]), 1, 0)


def _unchunks(t):
    t = jnp.moveaxis(t, 0, 1)
    return t.reshape((t.shape[0], t.shape[1] * t.shape[2]) + t.shape[3:])


def gla_chunk_scan(q, k, v, logf, s0):
    c = A_CHUNK
    causal = jnp.tril(jnp.ones((c, c), bool))[None, :, :, None, None]

    def step(s, inp):
        qc, kc, vc, gc = inp
        g = jnp.cumsum(gc, axis=1)
        seg = jnp.where(causal, g[:, :, None] - g[:, None, :], -jnp.inf)
        att = jnp.einsum('nthk,ntshk,nshk->ntsh', qc, jnp.exp(seg), kc)
        o = (jnp.einsum('ntsh,nshv->nthv', att, vc)
             + jnp.einsum('nthk,nhkv->nthv', qc * jnp.exp(g), s))
        g_last = g[:, -1]
        s = (jnp.exp(g_last)[..., None] * s
             + jnp.einsum('nshk,nshv->nhkv', kc * jnp.exp(g_last[:, None] - g), vc))
        return s, o

    s_fin, o = lax.scan(step, s0, (_chunks(q, c), _chunks(k, c), _chunks(v, c), _chunks(logf, c)))
    return _unchunks(o), s_fin


def gla_final_state(k, v, logf):
    g = jnp.cumsum(logf, axis=1)
    return jnp.einsum('nlhk,nlhv->nhkv', k * jnp.exp(g[:, -1:] - g), v)


def ssd_chunk_scan(x, dt, a, bm, cm, s0):
    c = B_CHUNK
    causal = jnp.tril(jnp.ones((c, c), bool))[None, :, :, None, None]

    def step(h, inp):
        xc, dtc, bc, cc = inp
        acum = jnp.cumsum(dtc * a[:, None], axis=1)
        seg = jnp.where(causal, acum[:, :, None] - acum[:, None, :], -jnp.inf)
        cb = jnp.einsum('ntgd,nsgd->ntsg', cc, bc)
        w = cb[..., None] * jnp.exp(seg) * dtc[:, None]
        y = (jnp.einsum('ntsgj,nsgjp->ntgjp', w, xc)
             + jnp.exp(acum)[..., None] * jnp.einsum('ntgd,ngjpd->ntgjp', cc, h))
        a_last = acum[:, -1]
        h = (jnp.exp(a_last)[..., None, None] * h
             + jnp.einsum('nsgj,nsgd,nsgjp->ngjpd', jnp.exp(a_last[:, None] - acum) * dtc, bc, xc))
        return h, y

    h_fin, y = lax.scan(step, s0, (_chunks(x, c), _chunks(dt, c), _chunks(bm, c), _chunks(cm, c)))
    return _unchunks(y), h_fin


def ssd_final_state(x, dt, a, bm):
    acum = jnp.cumsum(dt * a[:, None], axis=1)
    return jnp.einsum('nlgj,nlgd,nlgjp->ngjpd', jnp.exp(acum[:, -1:] - acum) * dt, bm, x)


def _hgrn2_kfv(p, lb):
    bl = p.shape[:2]
    f_raw = p[..., :2 * A_DF].astype(F32).reshape(bl + (2, A_HEADS, A_DK))
    lbb = lb.astype(F32).reshape(2, A_HEADS, A_DK)
    logf = jnp.log(lbb + (1.0 - lbb) * jax.nn.sigmoid(f_raw))
    k = (1.0 - lbb) * jax.nn.sigmoid(-f_raw)
    v = p[..., 2 * A_DF:].astype(F32).reshape(bl + (A_HEADS, A_DV))
    return (_bidir(k[:, :, 0], k[:, :, 1]), _bidir(logf[:, :, 0], logf[:, :, 1]), _bidir(v, v))


def _hgrn2_q(p):
    q = p[..., :A_DF].astype(F32).reshape(p.shape[:2] + (A_HEADS, A_DK))
    return _bidir(q, q)


def _hgrn2_out(o2, g, onorm_w, out_w):
    o = _merge(o2)
    o = o.reshape(o.shape[:2] + (D_INNER,))
    y = rmsnorm(o, onorm_w) * jax.nn.silu(g.astype(F32))
    return y.astype(g.dtype) @ out_w


def hgrn2_branch(h_lat, h_ctx, in_w, lb, onorm_w, out_w, need_ctx):
    b = h_lat.shape[0]
    kfv_cols = slice(A_DF, 3 * A_DF + D_INNER)
    g_start = 3 * A_DF + D_INNER
    if need_ctx:
        p_c = h_ctx @ in_w
        k_c, lf_c, v_c = _hgrn2_kfv(p_c[..., kfv_cols], lb)
        s0 = jnp.zeros((2 * b, A_HEADS, A_DK, A_DV), F32)
        o_c, s_ctx = gla_chunk_scan(_hgrn2_q(p_c), k_c, v_c, lf_c, s0)
        y_ctx = _hgrn2_out(o_c, p_c[..., g_start:], onorm_w, out_w)
    else:
        k_c, lf_c, v_c = _hgrn2_kfv(h_ctx @ in_w[:, kfv_cols], lb)
        s_ctx = gla_final_state(k_c, v_c, lf_c)
        y_ctx = None
    p = h_lat @ in_w
    k, lf, v = _hgrn2_kfv(p[..., kfv_cols], lb)
    o, _ = gla_chunk_scan(_hgrn2_q(p), k, v, lf, s_ctx)
    y_lat = _hgrn2_out(o, p[..., g_start:], onorm_w, out_w)
    return y_lat, y_ctx


def centred_dwconv_silu(t, w, bias):
    kw = w.shape[0]
    out = lax.conv_general_dilated(t, w[:, None, :].astype(t.dtype), window_strides=(1,),
                                   padding=[(kw // 2, kw // 2)],
                                   dimension_numbers=('NWC', 'WIO', 'NWC'),
                                   feature_group_count=t.shape[-1])
    return jax.nn.silu(out + bias.astype(t.dtype))


def _m2_x(t):
    return t.astype(F32).reshape(t.shape[:2] + (B_GROUPS, B_HPG, B_HEADDIM))


def _m2_bc(t):
    return t.astype(F32).reshape(t.shape[:2] + (B_GROUPS, B_STATE))


def _m2_dt(dt_raw, dt_bias):
    bl = dt_raw.shape[:2]
    dt = jax.nn.softplus(dt_raw.astype(F32).reshape(bl + (2, B_GROUPS, B_HPG))
                         + dt_bias.astype(F32).reshape(2, B_GROUPS, B_HPG))
    return _bidir(dt[:, :, 0], dt[:, :, 1])


def _m2_out(y2, xs, z, d_skip, onorm_w, out_w):
    y = _merge(y2) + d_skip.astype(F32).reshape(B_GROUPS, B_HPG, 1) * xs
    y = y.reshape(y.shape[:2] + (D_INNER,))
    y = rmsnorm(y * jax.nn.silu(z.astype(F32)), onorm_w)
    return y.astype(z.dtype) @ out_w


def mamba2_branch(h_lat, h_ctx, rows, in_w, conv_w, conv_b, dt_bias, a_log, d_skip, onorm_w, out_w, need_ctx):
    b = h_lat.shape[0]
    a = -jnp.exp(a_log.astype(F32)).reshape(2, B_GROUPS, B_HPG)
    a2 = jnp.repeat(a, b, axis=0)
    xbc_end = D_INNER + B_XBC
    if need_ctx:
        p_c = h_ctx @ in_w
        xbc = centred_dwconv_silu(p_c[..., D_INNER:xbc_end], conv_w, conv_b)
        xs_c = _m2_x(xbc[..., :D_INNER])
        bm_c = _m2_bc(xbc[..., D_INNER:D_INNER + B_GS])
        cm_c = _m2_bc(xbc[..., D_INNER + B_GS:])
        s0 = jnp.zeros((2 * b, B_GROUPS, B_HPG, B_HEADDIM, B_STATE), F32)
        y2_c, s_ctx = ssd_chunk_scan(_bidir(xs_c, xs_c), _m2_dt(p_c[..., xbc_end:], dt_bias), a2,
                                     _bidir(bm_c, bm_c), _bidir(cm_c, cm_c), s0)
        y_ctx = _m2_out(y2_c, xs_c, p_c[..., :D_INNER], d_skip, onorm_w, out_w)
    else:
        xb_w = D_INNER + B_GS
        xb = centred_dwconv_silu(h_ctx @ in_w[:, D_INNER:D_INNER + xb_w], conv_w[:, :xb_w], conv_b[:xb_w])
        xs_c = _m2_x(xb[..., :D_INNER])
        bm_c = _m2_bc(xb[..., D_INNER:])
        s_ctx = ssd_final_state(_bidir(xs_c, xs_c), _m2_dt(h_ctx @ in_w[:, xbc_end:], dt_bias), a2,
                                _bidir(bm_c, bm_c))
        y_ctx = None
    p = to_col_major(h_lat, rows) @ in_w
    xbc = centred_dwconv_silu(p[..., D_INNER:xbc_end], conv_w, conv_b)
    xs = _m2_x(xbc[..., :D_INNER])
    bm = _m2_bc(xbc[..., D_INNER:D_INNER + B_GS])
    cm = _m2_bc(xbc[..., D_INNER + B_GS:])
    y2, _ = ssd_chunk_scan(_bidir(xs, xs), _m2_dt(p[..., xbc_end:], dt_bias), a2,
                           _bidir(bm, bm), _bidir(cm, cm), s_ctx)
    y_lat = from_col_major(_m2_out(y2, xs, p[..., :D_INNER], d_skip, onorm_w, out_w), rows)
    return y_lat, y_ctx


def setup_inputs(seed: int = 0) -> dict:
    key = jax.random.key(seed)
    ks = jax.random.split(key, 24)
    n_a = (DEPTH + N_MIXERS - 1) // N_MIXERS
    n_b = DEPTH // N_MIXERS

    def nrm(k, shape, s):
        return jax.random.normal(k, shape, F32) * s

    dt0 = jnp.exp(jax.random.uniform(ks[14], (n_b, 2, B_HEADS), F32, np.log(1e-3), np.log(1e-1)))
    dt_bias = dt0 + jnp.log(-jnp.expm1(-dt0))
    a_log = jnp.log(jax.random.uniform(ks[15], (n_b, 2, B_HEADS), F32, 1.0, 16.0))
    return {
        "x": nrm(ks[0], (BATCH, SEQ, D_MODEL), 1.0),
        "c": nrm(ks[1], (BATCH, D_MODEL), 1.0),
        "ctx": nrm(ks[2], (BATCH, CTX_LEN, D_MODEL), 1.0),
        "c_ctx": nrm(ks[3], (D_MODEL,), 1.0),
        "norm_w": 1.0 + nrm(ks[4], (DEPTH, D_MODEL), 0.02),
        "mod_w": nrm(ks[5], (DEPTH, D_MODEL, 3 * D_MODEL), 0.5 * D_MODEL ** -0.5),
        "mod_b": nrm(ks[6], (DEPTH, 3 * D_MODEL), 0.02),
        "a_in_w": nrm(ks[7], (n_a, D_MODEL, A_IN_COLS), D_MODEL ** -0.5),
        "a_lb": nrm(ks[8], (n_a + 1, 2, A_DF), 0.5),
        "a_onorm_w": 1.0 + nrm(ks[9], (n_a, D_INNER), 0.02),
        "a_out_w": nrm(ks[10], (n_a, D_INNER, D_MODEL), D_INNER ** -0.5),
        "b_in_w": nrm(ks[11], (n_b, D_MODEL, B_IN_COLS), D_MODEL ** -0.5),
        "b_conv_w": nrm(ks[12], (n_b, B_CONV, B_XBC), B_CONV ** -0.5),
        "b_conv_b": nrm(ks[13], (n_b, B_XBC), 0.02),
        "b_dt_bias": dt_bias,
        "b_a_log": a_log,
        "b_d": 1.0 + nrm(ks[16], (n_b, B_HEADS), 0.1),
        "b_onorm_w": 1.0 + nrm(ks[17], (n_b, D_INNER), 0.02),
        "b_out_w": nrm(ks[18], (n_b, D_INNER, D_MODEL), D_INNER ** -0.5),
        "final_norm_w": 1.0 + nrm(ks[19], (D_MODEL,), 0.02),
    }


def reference(x, c, ctx, c_ctx, norm_w, mod_w, mod_b, a_in_w, a_lb, a_onorm_w, a_out_w,
              b_in_w, b_conv_w, b_conv_b, b_dt_bias, b_a_log, b_d, b_onorm_w, b_out_w, final_norm_w):
    rows = x.shape[1] // GRID_W
    lb_all = jnp.cumsum(jax.nn.softmax(a_lb.astype(F32), axis=0), axis=0)
    silu_c = jax.nn.silu(c)
    silu_cc = jax.nn.silu(c_ctx)
    for i in range(DEPTH):
        sh, sc, gt = jnp.split(silu_c @ mod_w[i] + mod_b[i], 3, axis=-1)
        sh_c, sc_c, gt_c = jnp.split(silu_cc @ mod_w[i] + mod_b[i], 3, axis=-1)
        h = rmsnorm(x, norm_w[i]) * (1.0 + sc[:, None]) + sh[:, None]
        h_c = rmsnorm(ctx, norm_w[i]) * (1.0 + sc_c) + sh_c
        need_ctx = i < DEPTH - 1
        j = i // N_MIXERS
        if i % N_MIXERS == 0:
            y, y_c = hgrn2_branch(h, h_c, a_in_w[j], lb_all[j], a_onorm_w[j], a_out_w[j], need_ctx)
        else:
            y, y_c = mamba2_branch(h, h_c, rows, b_in_w[j], b_conv_w[j], b_conv_b[j], b_dt_bias[j],
                                   b_a_log[j], b_d[j], b_onorm_w[j], b_out_w[j], need_ctx)
        x = x + gt[:, None] * y
        if need_ctx:
            ctx = ctx + gt_c * y_c
    return rmsnorm(x, final_norm_w)
```

```python
import os as _os
import numpy as np
from contextlib import ExitStack
import concourse.bass as bass
import concourse.mybir as mybir
from concourse.bass_utils import run_bass_kernel_spmd

F32 = mybir.dt.float32
BF16 = mybir.dt.bfloat16
AF = mybir.ActivationFunctionType
ALU = mybir.AluOpType

D = 2048
L = 2048
CT = 256
T = L + CT
KC = 16
DI = 4096
EPS = 1e-6
NT = T // 128
ENGS = ("pe", "act", "dve", "pool", "sp")
NSLOT = 8


class Prog:
    def __init__(self, nc, stack):
        self.nc = nc
        self.stack = stack
        self.ops = []
        self.last_w = {}
        self.readers = {}
        self.bar = set()
        self.last_eng = {}
        self.dma_recent = {e: [] for e in ENGS}

    def op(self, eng, fn, reads=(), writes=(), acc=False, dma=False):
        if getattr(self, "cap_active", False):
            self.capcount = getattr(self, "capcount", 0) + 1
            if self.capcount > self.cap:
                return None
        i = len(self.ops)
        pk_ = [k for k in reads if k[:2] in ("bk", "pb") and k not in writes]
        if pk_:
            writes = list(writes) + pk_
        deps = set(self.bar)
        for k in reads:
            if k in self.last_w:
                deps.add(self.last_w[k])
        for k in writes:
            if k in self.last_w:
                j = self.last_w[k]
                if not (acc and self.ops[j]["eng"] == eng and not self.ops[j]["dma"]):
                    deps.add(j)
            for j in self.readers.get(k, ()):
                deps.add(j)
        self.ops.append(dict(eng=eng, fn=fn, deps=deps, dma=dma, sig=False))
        for k in reads:
            self.readers.setdefault(k, []).append(i)
        for k in writes:
            self.last_w[k] = i
            self.readers[k] = []
        if dma:
            self.dma_recent[eng].append(i)
            self.dma_recent[eng] = self.dma_recent[eng][-NSLOT:]
        else:
            self.last_eng[eng] = i
        return i

    def barrier(self):
        b = set(self.last_eng.values())
        for e in ENGS:
            b.update(self.dma_recent[e])
        self.bar = b
        self.last_w = {}
        self.readers = {}

    def emit(self):
        nc = self.nc
        ops = self.ops
        for o in ops:
            for d in o["deps"]:
                ops[d]["sig"] = True
        sem_c = {e: self.stack.enter_context(nc.semaphore("c_" + e)) for e in ENGS if e != "sp"}
        sem_d = {e: [self.stack.enter_context(nc.semaphore("d_%s_%d" % (e, s))) for s in range(NSLOT)]
                 for e in ("sp", "act", "pool")}
        cnt = {e: 0 for e in ENGS}
        dcnt = {e: 0 for e in ENGS}
        for o in ops:
            e = o["eng"]
            if o["dma"]:
                n = dcnt[e]
                dcnt[e] += 1
                o["semh"] = sem_d[e][n % NSLOT]
                o["semkey"] = ("d", e, n % NSLOT)
                o["val"] = 16 * (n // NSLOT + 1)
                o["dn"] = n
            elif o["sig"]:
                cnt[e] += 1
                o["semh"] = sem_c[e]
                o["semkey"] = ("c", e)
                o["val"] = cnt[e]
        known = {e: {} for e in ENGS}
        streams = {e: [] for e in ENGS}
        dma_hist = {e: [] for e in ENGS}
        for o in ops:
            e = o["eng"]
            waits = {}
            for d in o["deps"]:
                od = ops[d]
                k = od["semkey"]
                if od["val"] > waits.get(k, (None, 0))[1]:
                    waits[k] = (od["semh"], od["val"])
            if o["dma"]:
                n = o["dn"]
                if n >= NSLOT:
                    prev = dma_hist[e][n - NSLOT]
                    k = prev["semkey"]
                    if prev["val"] > waits.get(k, (None, 0))[1]:
                        waits[k] = (prev["semh"], prev["val"])
                dma_hist[e].append(o)
            wl = []
            for k, (h, v) in waits.items():
                if known[e].get(k, 0) >= v:
                    continue
                known[e][k] = v
                wl.append((h, v))
            streams[e].append((o, wl))
        fin = []
        for e in ("sp", "act", "pool"):
            for o in dma_hist[e][-NSLOT:]:
                fin.append((o["semh"], o["val"]))
        self.n_instr = {e: len(streams[e]) for e in ENGS}
        self.sem_max = dict(cnt); self.dma_cnt = dict(dcnt)

        def run(eng_name):
            def body(engine):
                for o, wl in streams[eng_name]:
                    for h, v in wl:
                        engine.wait_ge(h, v)
                    ins = o["fn"](engine)
                    if o["dma"]:
                        ins.then_inc(o["semh"], 16)
                    elif o["sig"]:
                        ins.then_inc(o["semh"], 1)
                if eng_name == "sp":
                    for h, v in fin:
                        engine.wait_ge(h, v)
            return body

        with nc.Block() as block:
            block.tensor(run("pe"))
            block.scalar(run("act"))
            block.vector(run("dve"))
            block.gpsimd(run("pool"))
            block.sync(run("sp"))


class KB:
    def __init__(self, P):
        self.P = P

    def mm(self, out, lhsT, rhs, start=True, stop=True, r=(), w=(), acc=False):
        self.P.op("pe", lambda e: e.matmul(out, lhsT=lhsT, rhs=rhs, start=start, stop=stop), r, w, acc=acc)

    def tr(self, out, in_, ident, r=(), w=(), acc=False):
        self.P.op("pe", lambda e: e.transpose(out, in_, ident), r, w, acc=acc)

    def act(self, out, in_, func, bias=None, scale=None, accum=None, r=(), w=()):
        kw = {}
        if bias is not None:
            kw["bias"] = bias
        if scale is not None:
            kw["scale"] = scale
        if accum is not None:
            kw["accum_out"] = accum
        self.P.op("act", lambda e: e.activation(out=out, in_=in_, func=func, **kw), r, w)

    def tt(self, eng, out, in0, in1, op, r=(), w=()):
        self.P.op(eng, lambda e: e.tensor_tensor(out=out, in0=in0, in1=in1, op=op), r, w)

    def ts(self, eng, out, in0, s1, s2, op0, op1=None, r=(), w=()):
        if op1 is None:
            self.P.op(eng, lambda e: e.tensor_scalar(out=out, in0=in0, scalar1=s1, scalar2=None, op0=op0), r, w)
        else:
            self.P.op(eng, lambda e: e.tensor_scalar(out=out, in0=in0, scalar1=s1, scalar2=s2, op0=op0, op1=op1), r, w)

    def stt(self, eng, out, in0, scalar, in1, op0, op1, r=(), w=()):
        self.P.op(eng, lambda e: e.scalar_tensor_tensor(out=out, in0=in0, scalar=scalar, in1=in1, op0=op0, op1=op1), r, w)

    def cp(self, eng, out, in_, r=(), w=()):
        if eng == "act":
            self.P.op("act", lambda e: e.copy(out=out, in_=in_), r, w)
        else:
            self.P.op(eng, lambda e: e.tensor_copy(out=out, in_=in_), r, w)

    def memset(self, eng, out, val, w=()):
        self.P.op(eng, lambda e: e.memset(out, val), (), w)

    def scan(self, out, d0, d1, r=(), w=()):
        self.P.op("dve", lambda e: e.tensor_tensor_scan(out=out, data0=d0, data1=d1, initial=0.0,
                                                        op0=ALU.mult, op1=ALU.add), r, w)

    def recip(self, out, in_, r=(), w=()):
        self.P.op("dve", lambda e: e.reciprocal(out=out, in_=in_), r, w)

    def asel(self, out, in_, cmp, fill, base, pattern, cm, r=(), w=()):
        self.P.op("pool", lambda e: e.affine_select(out=out, in_=in_, compare_op=cmp, fill=fill, base=base,
                                                    pattern=pattern, channel_multiplier=cm), r, w)

    def dma(self, q, out, in_, r=(), w=(), slow=False):
        if slow:
            self.P.op(q, lambda e: e.dma_start(out=out, in_=in_, allow_slow_non_contiguous=True), r, w, dma=True)
        else:
            self.P.op(q, lambda e: e.dma_start(out=out, in_=in_), r, w, dma=True)


def build_nc(stop_after=None, debug=False):
    nc = bass.Bass("TRN2", target_bir_lowering=False)

    def din(name, shape):
        return nc.dram_tensor(name, shape, F32, kind="ExternalInput").ap()

    x_d = din("x", [L, D]); ctx_d = din("ctx", [CT, D]); c_d = din("c", [D]); cc_d = din("c_ctx", [D])
    norm_w_d = din("norm_w", [2, D]); mod_w_d = din("mod_w", [2, D, 3 * D]); mod_b_d = din("mod_b", [2, 3 * D])
    a_in_w_d = din("a_in_w", [D, 14336]); a_lb_d = din("a_lb", [2, 2, 2048]); a_on_d = din("a_onorm_w", [DI])
    a_out_w_d = din("a_out_w", [DI, D]); b_in_w_d = din("b_in_w", [D, 10368]); b_cw_d = din("b_conv_w", [5, 6144])
    b_cb_d = din("b_conv_b", [6144]); b_dtb_d = din("b_dt_bias", [128]); b_alog_d = din("b_a_log", [128])
    b_d_d = din("b_d", [64]); b_on_d = din("b_onorm_w", [DI]); b_out_w_d = din("b_out_w", [DI, D])
    fnw_d = din("final_norm_w", [D])
    out_d = nc.dram_tensor("out", [L, D], F32, kind="ExternalOutput").ap()
    x1_d = nc.dram_tensor("x1s", [L, D], F32, kind="ExternalOutput" if debug else "Internal").ap()
    ctx1_d = nc.dram_tensor("ctx1s", [CT, D], F32, kind="ExternalOutput" if debug else "Internal").ap()
    yg_d = nc.dram_tensor("ygs", [T, DI], BF16, kind="Internal").ap()
    gt_d = nc.dram_tensor("gts", [3, D], F32, kind="Internal").ap()
    rs_d = nc.dram_tensor("rss", [L], F32, kind="Internal").ap()

    with ExitStack() as st0:
        P = Prog(nc, st0)
        K = KB(P)
        global _LASTP
        _LASTP = P

        _uid = [0]

        def sbt(st, name, shape, dt=F32):
            _uid[0] += 1
            return st.enter_context(nc.sbuf_tensor("%s_u%d" % (name, _uid[0]), shape, dt))

        bank = [st0.enter_context(nc.psum_tensor("bank%d" % i, [128, 512], F32)) for i in range(6)]
        pbfs = [st0.enter_context(nc.psum_tensor("pbf%d" % i, [128, 1024], BF16)) for i in range(2)]

        ident32 = sbt(st0, "ident32", [128, 128]); ident = sbt(st0, "ident", [128, 128], BF16)
        maskF = sbt(st0, "maskF", [128, 128]); maskB = sbt(st0, "maskB", [128, 128])
        negF = sbt(st0, "negF", [128, 128], BF16); negB = sbt(st0, "negB", [128, 128], BF16)
        ones32 = sbt(st0, "ones32", [128, 128])
        lbT = sbt(st0, "lbT", [128, 32])
        modT = sbt(st0, "modT", [128, 2, 32, 2])
        Amod = sbt(st0, "Amod", [128, 2, 2, 16]); SHmod = sbt(st0, "SHmod", [128, 2, 2, 16])
        onT = sbt(st0, "onT", [128, 2, 32])
        cwT = sbt(st0, "cwT", [128, 48, 5]); cbT = sbt(st0, "cbT", [128, 48])
        dtb = sbt(st0, "dtb", [128, 1]); aneg = sbt(st0, "aneg", [128, 1])
        Dbc = sbt(st0, "Dbc", [128, 64])
        rstdo = sbt(st0, "rstdo", [128, NT])

        K.memset("pool", ident32[:], 0.0, w=["ident32"])
        K.asel(ident32[:], ident32[:], ALU.not_equal, 1.0, 0, [[-1, 128]], 1, r=["ident32"], w=["ident32"])
        K.cp("dve", ident[:], ident32[:], r=["ident32"], w=["ident"])
        K.memset("dve", ones32[:], 1.0, w=["ones32"])
        K.memset("pool", maskF[:], 1.0, w=["maskF"])
        K.asel(maskF[:], maskF[:], ALU.is_ge, 0.0, 0, [[1, 128]], -1, r=["maskF"], w=["maskF"])
        K.memset("pool", maskB[:], 1.0, w=["maskB"])
        K.asel(maskB[:], maskB[:], ALU.is_ge, 0.0, 0, [[-1, 128]], 1, r=["maskB"], w=["maskB"])
        K.memset("pool", negF[:], 0.0, w=["negF"])
        K.asel(negF[:], negF[:], ALU.is_ge, -30000.0, 0, [[1, 128]], -1, r=["negF"], w=["negF"])
        K.memset("pool", negB[:], 0.0, w=["negB"])
        K.asel(negB[:], negB[:], ALU.is_ge, 30000.0, 0, [[-1, 128]], 1, r=["negB"], w=["negB"])

        with ExitStack() as st:
            c2 = sbt(st, "c2", [128, 16, 2]); e2 = sbt(st, "e2", [128, 16, 2]); s2 = sbt(st, "s2", [128, 16, 2])
            lb0 = sbt(st, "lb0", [128, 32]); lb1 = sbt(st, "lb1", [128, 32])
            nwT = sbt(st, "nwT", [128, 2, 16]); modb = sbt(st, "modb", [1, 2, 3 * D])
            alg = sbt(st, "alg", [128, 1]); gtrow = sbt(st, "gtrow", [128, 512])
            slab = [sbt(st, "slab%d" % i, [128, 16, 512]) for i in range(2)]
            stage = sbt(st, "stage", [128, 128])
            ld_cnt = [0]

            def load_cm(dst_ap, src_rows_ap, J, dkey):
                i_ = ld_cnt[0]; ld_cnt[0] += 1
                K.dma("sp", stage[0:J, :], src_rows_ap, w=["stage"])
                pz = bank[5][:, 0:J]
                K.tr(pz, stage[0:J, :], ident32[0:J, 0:J], r=["stage", "ident32"], w=["bk5"])
                K.cp("dve", dst_ap, pz, r=["bk5"], w=[dkey])

            load_cm(c2[:, :, 0], c_d.rearrange("(j p) -> j p", p=128), 16, "c2")
            load_cm(c2[:, :, 1], cc_d.rearrange("(j p) -> j p", p=128), 16, "c2")
            load_cm(lb0[:], a_lb_d[0].rearrange("d (h p) -> (d h) p", p=128), 32, "lb0")
            load_cm(lb1[:], a_lb_d[1].rearrange("d (h p) -> (d h) p", p=128), 32, "lb1")
            load_cm(nwT[:].rearrange("p l j -> p (l j)"), norm_w_d.rearrange("l (j p) -> (l j) p", p=128), 32, "nwT")
            K.dma("sp", modb[:], mod_b_d.rearrange("(o l) c -> o l c", o=1), w=["modb"])
            load_cm(onT[:, 0, :], a_on_d.rearrange("(j p) -> j p", p=128), 32, "onT")
            load_cm(onT[:, 1, :], b_on_d.rearrange("(j p) -> j p", p=128), 32, "onT")
            for k_ in range(5):
                load_cm(cwT[:, :, k_], b_cw_d[k_].rearrange("(j p) -> j p", p=128), 48, "cwT")
            load_cm(cbT[:], b_cb_d.rearrange("(j p) -> j p", p=128), 48, "cbT")
            K.dma("sp", dtb[:], b_dtb_d.rearrange("(p o) -> p o", o=1), w=["dtb"])
            K.dma("sp", alg[:], b_alog_d.rearrange("(p o) -> p o", o=1), w=["alg"])
            K.dma("sp", Dbc[:], b_d_d.partition_broadcast(128), w=["Dbc"])
            K.act(aneg[:], alg[:], AF.Exp, r=["alg"], w=["aneg"])
            K.ts("dve", aneg[:], aneg[:], -1.0, None, ALU.mult, r=["aneg"], w=["aneg"])
            K.tt("dve", lb0[:], lb0[:], lb1[:], ALU.subtract, r=["lb0", "lb1"], w=["lb0"])
            K.act(lb1[:], lb0[:], AF.Exp, scale=-1.0, r=["lb0"], w=["lb1"])
            K.ts("dve", lb1[:], lb1[:], 1.0, None, ALU.add, r=["lb1"], w=["lb1"])
            K.recip(lbT[:], lb1[:], r=["lb1"], w=["lbT"])
            K.act(e2[:], c2[:], AF.Exp, scale=-1.0, r=["c2"], w=["e2"])
            K.ts("dve", e2[:], e2[:], 1.0, None, ALU.add, r=["e2"], w=["e2"])
            K.recip(e2[:], e2[:], r=["e2"], w=["e2"])
            K.tt("dve", s2[:], c2[:], e2[:], ALU.mult, r=["c2", "e2"], w=["s2"])
            pm = bank[0]
            for li in range(2):
                mw = mod_w_d[li].rearrange("(kc p) c -> p kc c", p=128)
                for s in range(12):
                    sl = slab[s % 2]; sk = "slab%d" % (s % 2)
                    K.dma("sp", sl[:], mw[:, :, s * 512:(s + 1) * 512], w=[sk])
                    if s < 8:
                        for j in range(4):
                            ch = s * 4 + j
                            po = pm[:, ch * 2:ch * 2 + 2]
                            for kc in range(KC):
                                K.mm(po, sl[:, kc, j * 128:(j + 1) * 128], s2[:, kc, :], start=(kc == 0), stop=False,
                                     r=[sk, "s2"], w=["bk0"], acc=(kc > 0))
                            K.mm(po, modb[0:1, li, ch * 128:(ch + 1) * 128], ones32[0:1, 0:2], start=False, stop=True,
                                 r=["modb", "ones32"], w=["bk0"], acc=True)
                        if s == 7:
                            K.cp("dve", modT[:, li, :, :], pm[:, 0:64].rearrange("p (c m) -> p c m", m=2), r=["bk0"], w=["modT"])
                    else:
                        nm = 2 if li == 0 else 1
                        for m in range(nm):
                            pg = bank[1 + m]; pk = "bk%d" % (1 + m)
                            for kc in range(KC):
                                K.mm(pg[:], s2[:, kc, m:m + 1].to_broadcast([128, 128]), sl[:, kc, :], start=(kc == 0), stop=False,
                                     r=[sk, "s2"], w=[pk], acc=(kc > 0))
                            K.mm(pg[:], ones32[0:1, 0:128], modb[0:1, li, s * 512:(s + 1) * 512], start=False, stop=True,
                                 r=["modb", "ones32"], w=[pk], acc=True)
                            K.cp("act", gtrow[:], pg[:], r=[pk], w=["gtrow"])
                            row = (0 if m == 0 else 1) if li == 0 else 2
                            K.dma("sp", gt_d[row:row + 1, (s - 8) * 512:(s - 7) * 512], gtrow[0:1, :], r=["gtrow"])
                for m in range(2):
                    K.stt("dve", Amod[:, li, m, :], modT[:, li, 16:32, m], 1.0, nwT[:, li, :], ALU.add, ALU.mult,
                          r=["modT", "nwT"], w=["Amod"])
                    K.cp("dve", SHmod[:, li, m, :], modT[:, li, 0:16, m], r=["modT"], w=["SHmod"])
        P.barrier()
        if stop_after == "p0":
            P.emit(); return nc

        def phaseA(st, li, hT, xsrc_d, csrc_d):
            xin = [sbt(st, "xin%d" % i, [128, D]) for i in range(3)]
            xn = [sbt(st, "xn%d" % i, [128, 4, D], BF16) for i in range(2)]
            junk = sbt(st, "junkA", [128, D], BF16)
            ssA = sbt(st, "ssA", [128, NT]); lnA = sbt(st, "lnA", [128, NT]); rsA = sbt(st, "rsA", [128, NT])
            groups = [(0, 2)] + [(1 + g, 4) for g in range(4)]
            tile_i = 0
            for gi, (g, ntile) in enumerate(groups):
                xg = xn[gi % 2]; xgk = "xn%d" % (gi % 2)
                m = 1 if g == 0 else 0
                for tl in range(ntile):
                    src = csrc_d[tl * 128:(tl + 1) * 128, :] if g == 0 else xsrc_d[((g - 1) * 4 + tl) * 128:((g - 1) * 4 + tl + 1) * 128, :]
                    xb = xin[tile_i % 3]; xk = "xin%d" % (tile_i % 3)
                    col = tile_i
                    K.dma("sp", xb[:], src, w=[xk])
                    K.act(junk[:], xb[:], AF.Square, accum=ssA[:, col:col + 1], r=[xk], w=["junkA", "ssA%d" % col])
                    K.act(lnA[:, col:col + 1], ssA[:, col:col + 1], AF.Ln, bias=EPS, scale=1.0 / D, r=["ssA%d" % col], w=["lnA%d" % col])
                    K.act(rsA[:, col:col + 1], lnA[:, col:col + 1], AF.Exp, scale=-0.5, r=["lnA%d" % col], w=["rsA%d" % col])
                    K.ts("pool" if tile_i % 2 else "dve", xg[:, tl, :], xb[:], rsA[:, col:col + 1], None, ALU.mult,
                         r=[xk, "rsA%d" % col], w=[xgk + "_%d" % tl])
                    tile_i += 1
                n = ntile * 128
                tok0 = 0 if g == 0 else CT + (g - 1) * 512
                for dc in range(KC):
                    pbf = pbfs[dc % 2]
                    pt = pbf[:, 0:n]; pk = "pb%d" % (dc % 2)
                    for tl in range(ntile):
                        K.tr(pbf[:, tl * 128:(tl + 1) * 128], xg[:, tl, dc * 128:(dc + 1) * 128], ident[:],
                             r=[xgk + "_%d" % tl, "ident"], w=[pk], acc=(tl > 0))
                    if li == 1 and g > 0:
                        o_ap = hT[:, dc, CT:T].rearrange("p (c r) -> p c r", r=32)[:, :, (g - 1) * 8:g * 8]
                        i_ap = pt.rearrange("p (r c) -> p c r", c=64)
                    else:
                        o_ap = hT[:, dc, tok0:tok0 + n]
                        i_ap = pt
                    hk = "hT%d_%d" % (dc, g)
                    if dc % 2 == 0:
                        K.act(o_ap, i_ap, AF.Identity, bias=SHmod[:, li, m, dc:dc + 1], scale=Amod[:, li, m, dc:dc + 1],
                              r=[pk, "Amod", "SHmod"], w=[hk])
                    else:
                        K.ts("dve", o_ap, i_ap, Amod[:, li, m, dc:dc + 1], SHmod[:, li, m, dc:dc + 1], ALU.mult, ALU.add,
                             r=[pk, "Amod", "SHmod"], w=[hk])

        GROUPS = [(0, 0, 256)] + [(1 + g, CT + g * 512, 512) for g in range(4)]

        def tile_group(t):
            return 0 if t < 2 else 1 + (t - 2) // 4

        def phaseC(st, li, ow_d, xsrc_d, csrc_d, ssum_key):
            owb = sbt(st, "owb", [128, 32, D], BF16)
            ygin = [sbt(st, "ygin%d" % i, [128, DI], BF16) for i in range(2)]
            ygT = [sbt(st, "ygT%d" % i, [128, 32, 128], BF16) for i in range(2)]
            xin = [sbt(st, "xinC%d" % i, [128, D]) for i in range(2)]
            xnew = [sbt(st, "xnew%d" % i, [128, D]) for i in range(1)]
            gtb = sbt(st, "gtb", [128, 2 if li == 0 else 1, D])
            if li == 1:
                fwb = sbt(st, "fwb", [128, D])
            ss2 = sbt(st, "ss2", [128, NT]); ln2 = sbt(st, "ln2", [128, NT]); rs2 = sbt(st, "rs2", [128, NT])
            owv = ow_d.rearrange("(kc p) c -> p kc c", p=128)
            for q in range(8):
                K.dma("pool", owb[:, q * 4:(q + 1) * 4, :], owv[:, q * 4:(q + 1) * 4, :], w=["owb%d" % q])
            if li == 0:
                K.dma("sp", gtb[:, 0, :], gt_d[0].partition_broadcast(128), w=["gtb"])
                K.dma("sp", gtb[:, 1, :], gt_d[1].partition_broadcast(128), w=["gtb"])
            else:
                K.dma("sp", gtb[:, 0, :], gt_d[2].partition_broadcast(128), w=["gtb"])
                K.dma("sp", fwb[:], fnw_d.partition_broadcast(128), w=["fwb"])
            tiles = list(range(NT)) if li == 0 else list(range(2, NT))
            ntl = len(tiles)

            def c_load(it):
                t = tiles[it]
                yb = ygin[it % 2]; yk = "ygin%d" % (it % 2)
                if li == 0:
                    K.dma("sp", yb[:], yg_d[t * 128:(t + 1) * 128, :], w=[yk])
                else:
                    ygv = yg_d[CT:T, :].rearrange("(c r) ch -> r c ch", r=32)
                    r0 = 2 * (t - 2)
                    K.dma("sp", yb[0:64, :], ygv[r0], w=[yk])
                    K.dma("sp", yb[64:128, :], ygv[r0 + 1], w=[yk])
                src = csrc_d[t * 128:(t + 1) * 128, :] if t < 2 else xsrc_d[(t - 2) * 128:(t - 1) * 128, :]
                K.dma("sp", xin[it % 2][:], src, w=["xinC%d" % (it % 2)])

            def c_transp(it, q):
                yb = ygin[it % 2]; yk = "ygin%d" % (it % 2)
                yt = ygT[it % 2]; ytk = "ygT%d" % (it % 2)
                half = q % 2
                for j in range(4):
                    kc = q * 4 + j
                    K.tr(pbfs[half][:, j * 128:(j + 1) * 128], yb[:, kc * 128:(kc + 1) * 128], ident[:],
                         r=[yk, "ident"], w=["pb%d" % half], acc=(j > 0))
                for j in range(4):
                    kc = q * 4 + j
                    src_ap = pbfs[half][:, j * 128:(j + 1) * 128]
                    if j % 2 == 0:
                        K.act(yt[:, kc, :], src_ap, AF.Identity, scale=onT[:, li, kc:kc + 1], r=["pb%d" % half, "onT"], w=[ytk + "_%d" % kc])
                    else:
                        K.ts("dve", yt[:, kc, :], src_ap, onT[:, li, kc:kc + 1], None, ALU.mult, r=["pb%d" % half, "onT"], w=[ytk + "_%d" % kc])

            def c_mm(it, db):
                t = tiles[it]
                yt = ygT[it % 2]; ytk = "ygT%d" % (it % 2)
                xb = xin[it % 2]; xk = "xinC%d" % (it % 2)
                xw = xnew[0]; xwk = "xnew0"
                m = 1 if t < 2 else 0
                po = bank[db]; pk = "bk%d" % db
                for kc in range(32):
                    K.mm(po[:], yt[:, kc, :], owb[:, kc, db * 512:(db + 1) * 512], start=(kc == 0), stop=(kc == 31),
                         r=[ytk + "_%d" % kc, "owb%d" % (kc // 4)], w=[pk], acc=(kc > 0))
                K.stt("dve", xw[:, db * 512:(db + 1) * 512], po[:], rstdo[:, t:t + 1], gtb[:, m, db * 512:(db + 1) * 512],
                      ALU.mult, ALU.mult, r=[pk, ssum_key, "gtb"], w=[xwk + "_%d" % db])
                K.tt("dve", xw[:, db * 512:(db + 1) * 512], xw[:, db * 512:(db + 1) * 512], xb[:, db * 512:(db + 1) * 512], ALU.add,
                     r=[xwk + "_%d" % db, xk], w=[xwk + "_%d" % db])

            def c_fin(it):
                t = tiles[it]
                xw = xnew[0]; xwk = "xnew0"
                wk = [xwk + "_%d" % db for db in range(4)]
                if li == 0:
                    dst = ctx1_d[t * 128:(t + 1) * 128, :] if t < 2 else x1_d[(t - 2) * 128:(t - 1) * 128, :]
                    K.dma("sp", dst, xw[:], r=wk)
                else:
                    K.act(ygin[it % 2][:, 0:D], xw[:], AF.Square, accum=ss2[:, t:t + 1], r=wk, w=["ygin%d" % (it % 2), "ss2_%d" % t])
                    K.act(ln2[:, t:t + 1], ss2[:, t:t + 1], AF.Ln, bias=EPS, scale=1.0 / D, r=["ss2_%d" % t], w=["ln2_%d" % t])
                    K.act(rs2[:, t:t + 1], ln2[:, t:t + 1], AF.Exp, scale=-0.5, r=["ln2_%d" % t], w=["rs2_%d" % t])
                    K.stt("dve", xw[:], xw[:], rs2[:, t:t + 1], fwb[:], ALU.mult, ALU.mult, r=wk + ["rs2_%d" % t, "fwb"], w=wk)
                    K.dma("sp", out_d[(t - 2) * 128:(t - 1) * 128, :], xw[:], r=wk)

            c_load(0)
            c_load(1)
            for q in range(8):
                c_transp(0, q)
            for it in range(ntl):
                for db in range(4):
                    c_mm(it, db)
                    if it + 1 < ntl:
                        c_transp(it + 1, 2 * db)
                        c_transp(it + 1, 2 * db + 1)
                c_fin(it)
                if it + 2 < ntl:
                    c_load(it + 2)

        def finish_ss(st, ss_all, nparts, ndiv):
            sst = sbt(st, "sst", [128, NT])
            P.op("dve", lambda e: e.tensor_reduce(out=sst[:], in_=ss_all[:], axis=mybir.AxisListType.X, op=ALU.add),
                 ["ss_all"], ["sst"])
            K.act(sst[:], sst[:], AF.Ln, bias=EPS, scale=1.0 / ndiv, r=["sst"], w=["sst"])
            K.act(rstdo[:], sst[:], AF.Exp, scale=-0.5, r=["sst"], w=["rstdo"])

        with ExitStack() as stL:
            hT = sbt(stL, "hT", [128, KC, T], BF16)
            with ExitStack() as st:
                phaseA(st, 0, hT, x_d, ctx_d)
            P.barrier()
            if stop_after == "a0":
                dbg_d = nc.dram_tensor("dbg", [128, KC * T], BF16, kind="ExternalOutput").ap()
                K.dma("sp", dbg_d, hT[:].rearrange("p k t -> p (k t)"), r=[])
                P.emit(); return nc
            with ExitStack() as st:
                W = sbt(st, "W", [128, KC, 896], BF16)
                qd = [sbt(st, "qd%d" % d, [128, T], BF16) for d in range(2)]
                kd = [sbt(st, "kd%d" % d, [128, T], BF16) for d in range(2)]
                qs = [sbt(st, "qs%d" % d, [128, T], BF16) for d in range(2)]
                ketm = [sbt(st, "ketm%d" % d, [128, NT, 128], BF16) for d in range(2)]
                vtm = sbt(st, "vtm", [128, NT, 256], BF16)
                sg = sbt(st, "sg", [128, NT, 256], BF16)
                oacc = sbt(st, "oacc", [128, NT, 256], BF16)
                q16 = [sbt(st, "q16_%d" % i, [128, 512], BF16) for i in range(2)]
                tA = [sbt(st, "tA%d" % d, [128, 512]) for d in range(2)]
                tB = [sbt(st, "tB%d" % d, [128, 512]) for d in range(2)]
                tC = [sbt(st, "tC%d" % d, [128, 512]) for d in range(2)]
                k32 = [sbt(st, "k32%d" % d, [128, 512]) for d in range(2)]
                keT = [sbt(st, "keT%d" % d, [128, 512], BF16) for d in range(2)]
                lo4 = sbt(st, "lo", [128, 2, 2, 8]); hi4 = sbt(st, "hi", [128, 2, 2, 8]); mi4 = sbt(st, "mi", [128, 2, 2, 8])
                rq4 = sbt(st, "rq", [128, 2, 2, 8]); rk4 = sbt(st, "rk", [128, 2, 2, 8])
                aall = sbt(st, "aall", [128, 2, 36])
                S32 = [sbt(st, "S32_%d" % d, [128, 256]) for d in range(2)]
                Sbf = [sbt(st, "Sbf_%d" % d, [128, 6, 256], BF16) for d in range(2)]
                amask = [sbt(st, "amask%d" % d, [128, 64]) for d in range(2)]
                attm = [sbt(st, "attm%d" % d, [128, 2, 64], BF16) for d in range(2)]
                o32 = [sbt(st, "o32_%d" % i, [128, 256]) for i in range(2)]
                ygt = [sbt(st, "ygt%d" % i, [128, 256], BF16) for i in range(2)]
                junk = sbt(st, "junkB", [128, 256], BF16)
                sgt = [sbt(st, "sgt%d" % i, [128, 256]) for i in range(2)]
                g32 = [sbt(st, "g32_%d" % i, [128, 256]) for i in range(2)]
                ss_all = sbt(st, "ss_all", [128, NT, 16])
                K.memset("pool", ss_all[:], 0.0, w=["ss_all"])
                for d_ in range(2):
                    mk_ = maskF if d_ == 0 else maskB
                    K.cp("dve", amask[d_][0:64, :], mk_[0:64, 0:64], r=["maskF", "maskB"], w=["amask"])
                    K.cp("dve", amask[d_][64:128, :], mk_[64:128, 64:128], r=["maskF", "maskB", "amask"], w=["amask"])
                wv = a_in_w_d.rearrange("(kc p) c -> p kc c", p=128)

                def load_w(h):
                    K.dma("pool", W[:, :, 0:128], wv[:, :, h * 128:(h + 1) * 128], w=["Wq"])
                    K.dma("pool", W[:, :, 128:256], wv[:, :, 2048 + h * 128:2048 + (h + 1) * 128], w=["Wf0"])
                    K.dma("pool", W[:, :, 256:384], wv[:, :, 4096 + h * 128:4096 + (h + 1) * 128], w=["Wf1"])
                    K.dma("pool", W[:, :, 384:640], wv[:, :, 6144 + h * 256:6144 + (h + 1) * 256], w=["Wvg"])
                    K.dma("pool", W[:, :, 640:896], wv[:, :, 10240 + h * 256:10240 + (h + 1) * 256], w=["Wvg"])

                P.cap = int(_os.environ.get("B0CAP", "100000000")); P.cap_active = True
                load_w(0)
                fw_order = list(range(NT))
                bw_order = [1, 0] + list(range(NT - 1, 1, -1))
                yg_cnt = [0]
                for h in range(int(_os.environ.get("NHEADS", "16"))):
                    for (g, tok0, n) in GROUPS[0:int(_os.environ.get("NGROUPS", "5"))]:
                        nch = n // 64
                        hk = ["hT%d_%d" % (kc, g) for kc in range(KC)]
                        pq = bank[0]
                        for kc in range(KC):
                            K.mm(pq[:, 0:n], W[:, kc, 0:128], hT[:, kc, tok0:tok0 + n], start=(kc == 0), stop=(kc == KC - 1),
                                 r=["Wq", hk[kc]], w=["bk0"], acc=(kc > 0))
                        q32 = q16[g % 2]; q32k = "q16_%d" % (g % 2)
                        K.cp("act", q32[:, 0:n], pq[:, 0:n], r=["bk0"], w=[q32k])
                        fdone = [False, False]

                        def fproj(d):
                            pf = bank[1 + d]; pfk = "bk%d" % (1 + d)
                            wk = "Wf%d" % d
                            for kc in range(KC):
                                K.mm(pf[:, 0:n], W[:, kc, 128 + d * 128:256 + d * 128], hT[:, kc, tok0:tok0 + n], start=(kc == 0), stop=(kc == KC - 1),
                                     r=[wk, hk[kc]], w=[pfk], acc=(kc > 0))
                                yield
                            fdone[d] = True

                        def chain(d, hf):
                            while not fdone[d]:
                                yield
                            m_ = n // 2; off = hf * m_; nchh = m_ // 64; co = off // 64
                            pf = bank[1 + d]; pfk = "bk%d" % (1 + d)
                            a_, b_, c_ = tA[d], tB[d], tC[d]
                            ak, bk, ck, kk = "tA%d_%d" % (d, hf), "tB%d_%d" % (d, hf), "tC%d_%d" % (d, hf), "k32%d_%d" % (d, hf)
                            lbc = lbT[:, d * 16 + h:d * 16 + h + 1]
                            e_eng = "dve"
                            K.act(a_[:, off:off + m_], pf[:, off:off + m_], AF.Exp, scale=-1.0, r=[pfk], w=[ak])
                            yield
                            K.act(b_[:, off:off + m_], a_[:, off:off + m_], AF.Ln, bias=1.0, scale=lbc, r=[ak, "lbT"], w=[bk])
                            yield
                            K.act(c_[:, off:off + m_], a_[:, off:off + m_], AF.Ln, bias=1.0, r=[ak], w=[ck])
                            yield
                            K.tt(e_eng, b_[:, off:off + m_], b_[:, off:off + m_], c_[:, off:off + m_], ALU.subtract, r=[bk, ck], w=[bk])
                            yield
                            K.act(c_[:, off:off + m_], b_[:, off:off + m_], AF.Exp, r=[bk], w=[ck])
                            yield
                            K.ts(e_eng, k32[d][:, off:off + m_], c_[:, off:off + m_], -1.0, 1.0, ALU.mult, ALU.add, r=[ck], w=[kk])
                            yield
                            K.scan(a_[:, off:off + m_], ones32[:, 0:1].to_broadcast([128, m_]), b_[:, off:off + m_], r=["ones32", bk], w=[ak])
                            yield
                            a3 = a_[:, off:off + m_].rearrange("p (c t) -> p c t", t=64)
                            b3 = b_[:, off:off + m_].rearrange("p (c t) -> p c t", t=64)
                            sk_ = "sc%d_%d_%d" % (d, g % 2, hf)
                            lo = lo4[:, g % 2]; hi = hi4[:, g % 2]; mi = mi4[:, g % 2]; rq = rq4[:, g % 2]; rk = rk4[:, g % 2]
                            if d == 0:
                                K.tt("dve", lo[:, d, co:co + nchh], a3[:, :, 0], b3[:, :, 0], ALU.subtract, r=[ak, bk], w=[sk_ + "lo"])
                                yield
                                K.cp("dve", hi[:, d, co:co + nchh], a3[:, :, 63], r=[ak], w=[sk_ + "hi"])
                                yield
                            else:
                                K.cp("dve", hi[:, d, co:co + nchh], a3[:, :, 63], r=[ak], w=[sk_ + "hi"])
                                yield
                                K.tt("dve", a_[:, off:off + m_], a_[:, off:off + m_], b_[:, off:off + m_], ALU.subtract, r=[ak, bk], w=[ak])
                                yield
                                K.cp("dve", lo[:, d, co:co + nchh], a3[:, :, 0], r=[ak], w=[sk_ + "lo"])
                                yield
                            K.cp("dve", mi[:, d, co:co + nchh], a3[:, :, 31], r=[ak], w=[sk_ + "mi"])
                            yield
                            gc0 = (tok0 // 64)
                            K.tt("dve", rq[:, d, co:co + nchh], mi[:, d, co:co + nchh] if d == 0 else hi[:, d, co:co + nchh],
                                 lo[:, d, co:co + nchh] if d == 0 else mi[:, d, co:co + nchh], ALU.subtract,
                                 r=[sk_ + "lo", sk_ + "hi", sk_ + "mi"], w=[sk_ + "rq"])
                            yield
                            K.tt("dve", rk[:, d, co:co + nchh], hi[:, d, co:co + nchh] if d == 0 else mi[:, d, co:co + nchh],
                                 mi[:, d, co:co + nchh] if d == 0 else lo[:, d, co:co + nchh], ALU.subtract,
                                 r=[sk_ + "lo", sk_ + "hi", sk_ + "mi"], w=[sk_ + "rk"])
                            yield
                            K.tt("dve", aall[:, d, gc0 + co:gc0 + co + nchh], hi[:, d, co:co + nchh], lo[:, d, co:co + nchh], ALU.subtract,
                                 r=[sk_ + "lo", sk_ + "hi"], w=["aall%d_%d_%d" % (d, g, hf)])
                            yield
                            K.act(rq[:, d, co:co + nchh], rq[:, d, co:co + nchh], AF.Exp, r=[sk_ + "rq"], w=[sk_ + "rq"])
                            yield
                            K.act(rk[:, d, co:co + nchh], rk[:, d, co:co + nchh], AF.Exp, r=[sk_ + "rk"], w=[sk_ + "rk"])
                            yield
                            K.act(aall[:, d, gc0 + co:gc0 + co + nchh], aall[:, d, gc0 + co:gc0 + co + nchh], AF.Exp, r=["aall%d_%d_%d" % (d, g, hf)], w=["aall%d_%d_%d" % (d, g, hf)])
                            yield
                            K.tt(e_eng, a3, a3, mi[:, d, co:co + nchh].unsqueeze(2).to_broadcast([128, nchh, 64]), ALU.subtract,
                                 r=[ak, sk_ + "mi"], w=[ak])
                            yield
                            K.act(b_[:, off:off + m_], a_[:, off:off + m_], AF.Exp, r=[ak], w=[bk])
                            yield
                            K.act(c_[:, off:off + m_], a_[:, off:off + m_], AF.Exp, scale=-1.0, r=[ak], w=[ck])
                            yield
                            qf, kf = (b_, c_) if d == 0 else (c_, b_)
                            qfk, kfk = (bk, ck) if d == 0 else (ck, bk)
                            qdk, kdk, qsk = "qd%d_%d_%d" % (d, g, hf), "kd%d_%d_%d" % (d, g, hf), "qs%d_%d_%d" % (d, g, hf)
                            K.tt("dve", qd[d][:, tok0 + off:tok0 + off + m_], q32[:, off:off + m_], qf[:, off:off + m_], ALU.mult, r=[q32k, qfk], w=[qdk])
                            yield
                            K.tt("pool", kd[d][:, tok0 + off:tok0 + off + m_], k32[d][:, off:off + m_], kf[:, off:off + m_], ALU.mult, r=[kk, kfk], w=[kdk])
                            yield
                            K.tt("dve", qs[d][:, tok0 + off:tok0 + off + m_].rearrange("p (c t) -> p c t", t=64),
                                 qd[d][:, tok0 + off:tok0 + off + m_].rearrange("p (c t) -> p c t", t=64),
                                 rq[:, d, co:co + nchh].unsqueeze(2).to_broadcast([128, nchh, 64]), ALU.mult, r=[qdk, sk_ + "rq"], w=[qsk])
                            yield
                            K.tt("pool", keT[d][:, off:off + m_].rearrange("p (c t) -> p c t", t=64),
                                 kd[d][:, tok0 + off:tok0 + off + m_].rearrange("p (c t) -> p c t", t=64),
                                 rk[:, d, co:co + nchh].unsqueeze(2).to_broadcast([128, nchh, 64]), ALU.mult, r=[kdk, sk_ + "rk"], w=["keT%d_%d" % (d, hf)])
                            yield
                            for tl in range(off // 128, (off + m_) // 128):
                                t = tok0 // 128 + tl
                                half = d
                                K.tr(pbfs[half][:, tl * 128:(tl + 1) * 128], keT[d][:, tl * 128:(tl + 1) * 128], ident[:],
                                     r=["keT%d_%d" % (d, hf), "ident"], w=["pb%d" % half], acc=(tl > off // 128))
                                yield
                            t0 = tok0 // 128
                            K.cp("act", ketm[d][:, t0 + off // 128:t0 + (off + m_) // 128, :], pbfs[d][:, off:off + m_].rearrange("p (t k) -> p t k", k=128),
                                 r=["pb%d" % d], w=["ketm%d_%d_%d" % (d, g, hf)])
                            yield
                        def vg():
                            for tl in range(n // 128):
                                t = tok0 // 128 + tl
                                bi_ = 3 + (t % 3)
                                pt = bank[bi_]; ptk = "bk%d" % bi_
                                for kc in range(KC):
                                    K.mm(pt[:], hT[:, kc, t * 128:(t + 1) * 128], W[:, kc, 384:896], start=(kc == 0), stop=(kc == KC - 1),
                                         r=["Wvg", hk[kc]], w=[ptk], acc=(kc > 0))
                                    yield
                                i_ = t % 2
                                K.cp("dve", vtm[:, t, :], pt[:, 0:256], r=[ptk], w=["vtm%d" % t])
                                yield
                                K.cp("act", g32[i_][:], pt[:, 256:512], r=[ptk], w=["g32_%d" % i_])
                                yield
                                K.act(sgt[i_][:], g32[i_][:], AF.Exp, scale=-1.0, r=["g32_%d" % i_], w=["sgt%d" % i_])
                                yield
                                K.ts("dve", sgt[i_][:], sgt[i_][:], 1.0, None, ALU.add, r=["sgt%d" % i_], w=["sgt%d" % i_])
                                yield
                                K.recip(sgt[i_][:], sgt[i_][:], r=["sgt%d" % i_], w=["sgt%d" % i_])
                                yield
                                K.tt("pool", sg[:, t, :], g32[i_][:], sgt[i_][:], ALU.mult, r=["g32_%d" % i_, "sgt%d" % i_], w=["sg%d" % t])
                                yield
                        gens = [fproj(0), fproj(1), chain(0, 0), chain(1, 0), chain(0, 1), chain(1, 1), vg()]
                        if _os.environ.get("NOILV"):
                            for g_ in gens:
                                for _ in g_:
                                    pass
                            gens = []
                        while gens:
                            for g_ in list(gens):
                                try:
                                    next(g_)
                                except StopIteration:
                                    gens.remove(g_)
                    if _os.environ.get("B0STOP") == "proj":
                        break
                    if h + 1 < 16:
                        load_w(h + 1)
                    RS = 6
                    for d in range(2):
                        K.memset("pool", S32[d][:], 0.0, w=["S32_%d" % d])
                        K.memset("pool", Sbf[d][:, 0, :], 0.0, w=["Sbf%d_0" % d])
                    visited = set()

                    def step_info(step):
                        info = []
                        for d in range(2):
                            t = fw_order[step] if d == 0 else bw_order[step]
                            chunks = (2 * t, 2 * t + 1) if d == 0 else (2 * t + 1, 2 * t)
                            info.append((d, t, tile_group(t), chunks))
                        return info

                    def scan_front(step):
                        info = step_info(step)
                        ab = step % 2
                        for (d, t, g, chunks) in info:
                            for c in chunks:
                                hp = c % 2; rows = slice(hp * 64, hp * 64 + 64); csl = slice(c * 64, c * 64 + 64)
                                K.mm(bank[d][rows, 0:64], kd[d][:, csl], qd[d][:, csl],
                                     r=["kd%d_%d_0" % (d, g), "kd%d_%d_1" % (d, g), "qd%d_%d_0" % (d, g), "qd%d_%d_1" % (d, g)], w=["bk%d" % d], acc=(c != chunks[0]))
                        for (d, t, g, chunks) in info:
                            K.tt("dve", attm[d][:, ab, :], bank[d][:, 0:64], amask[d][:, :], ALU.mult,
                                 r=["bk%d" % d, "amask"], w=["attm%d_%d" % (d, ab)])
                        for (d, t, g, chunks) in info:
                            for c in chunks:
                                hp = c % 2; rows = slice(hp * 64, hp * 64 + 64)
                                K.mm(bank[4 + d][:, hp * 256:(hp + 1) * 256], ketm[d][rows, t, :], vtm[rows, t, :],
                                     r=["ketm%d_%d_0" % (d, g), "ketm%d_%d_1" % (d, g), "vtm%d" % t], w=["bk%d" % (4 + d)])
                        for (d, t, g, chunks) in info:
                            for ci, c in enumerate(chunks):
                                hp = c % 2
                                slot = (2 * step + ci + 1) % RS
                                K.stt("dve", S32[d][:], S32[d][:], aall[:, d, c:c + 1], bank[4 + d][:, hp * 256:(hp + 1) * 256], ALU.mult, ALU.add,
                                      r=["S32_%d" % d, "aall%d_%d_0" % (d, g), "aall%d_%d_1" % (d, g), "bk%d" % (4 + d)], w=["S32_%d" % d])
                                K.cp("act", Sbf[d][:, slot, :], S32[d][:], r=["S32_%d" % d], w=["Sbf%d_%d" % (d, slot)])

                    def scan_back(step):
                        nonlocal_cnt = None
                        info = step_info(step)
                        ab = step % 2
                        for (d, t, g, chunks) in info:
                            po = bank[2 + d][:, 0:256]; pok = "bk%d" % (2 + d)
                            for ci, c in enumerate(chunks):
                                hp = c % 2; rows = slice(hp * 64, hp * 64 + 64); csl = slice(c * 64, c * 64 + 64)
                                slot = (2 * step + ci) % RS
                                K.mm(po[rows, :], attm[d][rows, ab, :], vtm[rows, t, :], start=True, stop=False,
                                     r=["attm%d_%d" % (d, ab), "vtm%d" % t], w=[pok], acc=(ci > 0))
                                K.mm(po[rows, :], qs[d][:, csl], Sbf[d][:, slot, :], start=False, stop=True,
                                     r=["qs%d_%d_0" % (d, g), "qs%d_%d_1" % (d, g), "Sbf%d_%d" % (d, slot)], w=[pok], acc=True)
                            if t not in visited:
                                visited.add(t)
                                K.cp("act", oacc[:, t, :], po, r=[pok], w=["oacc%d" % t])
                            else:
                                i2 = yg_cnt[0] % 2; yg_cnt[0] += 1
                                K.tt("dve", o32[i2][:], po, oacc[:, t, :], ALU.add,
                                     r=[pok, "oacc%d" % t], w=["o32_%d" % i2])
                                K.act(junk[:], o32[i2][:], AF.Square, accum=ss_all[:, t, h:h + 1], r=["o32_%d" % i2], w=["junkB", "ss_all"])
                                K.tt("pool", ygt[i2][:], o32[i2][:], sg[:, t, :], ALU.mult, r=["o32_%d" % i2, "sg%d" % t], w=["ygt%d" % i2])
                                K.dma("sp", yg_d[t * 128:(t + 1) * 128, h * 256:(h + 1) * 256], ygt[i2][:], r=["ygt%d" % i2])

                    scan_front(0)
                    for step in range(NT):
                        if step + 1 < NT:
                            scan_front(step + 1)
                        scan_back(step)
                P.cap_active = False
                finish_ss(st, ss_all, NT, float(DI))
            P.barrier()
        if stop_after == "b0":
            P.emit(); return nc
        with ExitStack() as st:
            phaseC(st, 0, a_out_w_d, x_d, ctx_d, "rstdo")
        P.barrier()
        if stop_after == "c0":
            P.emit(); return nc

        with ExitStack() as stL:
            hT = sbt(stL, "hT1", [128, KC, T], BF16)
            with ExitStack() as st:
                phaseA(st, 1, hT, x1_d, ctx1_d)
            P.barrier()
            if stop_after == "a1":
                dbg_d = nc.dram_tensor("dbg", [128, KC * T], BF16, kind="ExternalOutput").ap()
                K.dma("sp", dbg_d, hT[:].rearrange("p k t -> p (k t)"), r=[])
                P.emit(); return nc
            with ExitStack() as st:
                wv = b_in_w_d.rearrange("(kc p) c -> p kc c", p=128)
                Wr = sbt(st, "Wr", [128, KC, 512], BF16)
                biasTM = sbt(st, "biasTM", [128, NT, 128])
                sOutTM = sbt(st, "sOutTM", [128, NT, 128], BF16)
                wSTM = sbt(st, "wSTM", [128, NT, 128], BF16)
                abc = sbt(st, "abc", [128, NT, 128])
                QQ = [sbt(st, "QQ%d" % d, [128, T], BF16) for d in range(2)]
                id2 = sbt(st, "id2", [128, 64], BF16)
                onesb = sbt(st, "onesb1", [128, 128])
                ncbT = sbt(st, "ncbT", [128, 48])
                K.memset("dve", onesb[:], 1.0, w=["onesb"])
                K.ts("dve", ncbT[:], cbT[:], -1.0, None, ALU.mult, r=["cbT"], w=["ncbT"])
                CH = [(c, c * 128) for c in range(NT)]
                with ExitStack() as sp_:
                    DT = sbt(sp_, "DT", [128, T]); lnD = sbt(sp_, "lnD", [128, T]); dtA = sbt(sp_, "dtA", [128, T])
                    PP = sbt(sp_, "PP", [128, T]); YY = sbt(sp_, "YY", [128, T]); bT = sbt(sp_, "bT", [128, T])
                    sOT = lnD; wST = DT; tmpP = dtA
                    Ptot = sbt(sp_, "Ptot", [128, NT]); aC = sbt(sp_, "aC", [128, NT]); Z = sbt(sp_, "Z", [128, NT, 128])
                    Qhi = sbt(sp_, "Qhi", [128, T], BF16); Qlo = sbt(sp_, "Qlo", [128, T], BF16)
                    K.dma("pool", Wr[:, :, 0:128], wv[:, :, 10240:10368], w=["Wr0"])
                    for (g, tok0, n) in GROUPS:
                        pf = bank[g % 2][:, 0:n]; pfk = "bk%d" % (g % 2)
                        for kc in range(KC):
                            K.mm(pf, Wr[:, kc, 0:128], hT[:, kc, tok0:tok0 + n], start=(kc == 0), stop=(kc == KC - 1),
                                 r=["Wr0"], w=[pfk], acc=(kc > 0))
                        K.act(DT[:, tok0:tok0 + n], pf, AF.Exp, bias=dtb[:, 0:1], r=[pfk, "dtb"], w=["DT%d" % g])
                        K.act(DT[:, tok0:tok0 + n], DT[:, tok0:tok0 + n], AF.Ln, bias=1.0, r=["DT%d" % g], w=["DT%d" % g])
                    dk = ["DT%d" % g for g in range(5)]
                    K.act(lnD[:], DT[:], AF.Ln, r=dk, w=["lnD"])
                    K.ts("pool", dtA[:], DT[:], aneg[:, 0:1], None, ALU.mult, r=dk + ["aneg"], w=["dtA"])
                    for (c, t0) in CH:
                        K.scan(PP[:, t0:t0 + 128], onesb[:, 0:128], dtA[:, t0:t0 + 128], r=["onesb", "dtA"], w=["PP%d" % c])
                    pk_ = ["PP%d" % c for c in range(NT)]
                    K.cp("pool", YY[:], PP[:], r=pk_, w=["YY"])
                    K.tt("pool", YY[64:128, :], PP[64:128, :], dtA[64:128, :], ALU.subtract, r=pk_ + ["dtA", "YY"], w=["YY"])
                    K.cp("pool", Ptot[:], PP[:].rearrange("p (c t) -> p c t", t=128)[:, :, 127], r=pk_, w=["Ptot"])
                    K.tt("dve", bT[0:64, :], lnD[0:64, :], PP[0:64, :], ALU.subtract, r=["lnD"] + pk_, w=["bT"])
                    K.tt("dve", bT[64:128, :], lnD[64:128, :], YY[64:128, :], ALU.add, r=["lnD", "YY", "bT"], w=["bT"])
                    ptb = Ptot[:].unsqueeze(2).to_broadcast([128, NT, 128])
                    K.act(sOT[0:64, :], PP[0:64, :], AF.Exp, r=pk_, w=["lnD"])
                    K.tt("pool", tmpP[:].rearrange("p (c t) -> p c t", t=128), YY[:].rearrange("p (c t) -> p c t", t=128), ptb,
                         ALU.subtract, r=["YY", "Ptot"], w=["dtA"])
                    K.act(sOT[64:128, :], tmpP[64:128, :], AF.Exp, scale=-1.0, r=["dtA", "lnD"], w=["lnD"])
                    K.tt("pool", tmpP[:].rearrange("p (c t) -> p c t", t=128), bT[:].rearrange("p (c t) -> p c t", t=128), ptb,
                         ALU.add, r=["bT", "Ptot", "dtA"], w=["dtA"])
                    K.act(wST[0:64, :], tmpP[0:64, :], AF.Exp, r=["dtA"], w=dk)
                    K.act(wST[64:128, :], bT[64:128, :], AF.Exp, r=["bT"] + dk, w=dk)
                    K.act(aC[:], Ptot[:], AF.Exp, r=["Ptot"], w=["aC"])
                    for (c, t0) in CH:
                        K.ts("pool" if c % 2 else "dve", Z[:, c, :], ident32[:], aC[:, c:c + 1], None, ALU.mult, r=["ident32", "aC"], w=["Z%d" % c])
                    Z2 = Z[:].rearrange("p c r -> p (c r)"); abc2 = abc[:].rearrange("p c r -> p (c r)")
                    for q in range(5):
                        w_ = min(512, NT * 128 - q * 512)
                        pz = bank[2 + q % 2][:, 0:w_]; pzk = "bk%d" % (2 + q % 2)
                        K.mm(pz, ones32[:], Z2[:, q * 512:q * 512 + w_], r=["ones32"] + ["Z%d" % c for c in range(NT)], w=[pzk])
                        K.cp("act", abc2[:, q * 512:q * 512 + w_], pz, r=[pzk], w=["abc"])
                    K.cp("dve", Qhi[:], YY[:], r=["YY"], w=["Qhi"])
                    K.tt("dve", Qlo[:], YY[:], Qhi[:], ALU.subtract, r=["YY", "Qhi"], w=["Qlo"])
                    K.tt("dve", id2[:], ident[:, 0:64], ident[:, 64:128], ALU.add, r=["ident"], w=["id2"])
                    K.cp("pool", QQ[0][0:64, :], Qhi[0:64, :], r=["Qhi"], w=["QQ0a"])
                    K.cp("pool", QQ[1][64:128, :], Qlo[64:128, :], r=["Qlo"], w=["QQ1b"])
                    K.dma("sp", QQ[0][64:128, :], Qlo[0:64, :], r=["Qlo"], w=["QQ0b"])
                    K.dma("sp", QQ[1][0:64, :], Qhi[64:128, :], r=["Qhi"], w=["QQ1a"])
                    for (c, t0) in CH:
                        for ai, (src, dst, dk_) in enumerate(((bT, biasTM, "biasTM"), (sOT, sOutTM, "sOutTM"), (wST, wSTM, "wSTM"))):
                            pb_ = bank[4 + (ai + c) % 2][:, 0:128]; pbk = "bk%d" % (4 + (ai + c) % 2)
                            K.tr(pb_, src[:, t0:t0 + 128], ident32[:], r=["bT", "lnD", "ident32"] + dk, w=[pbk])
                            K.cp("act" if (ai + c) % 2 else "dve", dst[:, c, :], pb_, r=[pbk], w=[dk_])
                P.barrier()
                B_tm = sbt(st, "B_tm", [128, NT, 128], BF16); BT = sbt(st, "BT", [128, T], BF16); CTt = sbt(st, "CTt", [128, T], BF16)
                x_tm = sbt(st, "x_tm", [128, NT, 256], BF16); sz = sbt(st, "sz", [128, 16, 256], BF16)
                Yint = sbt(st, "Yint", [128, 16, 256], BF16)
                rawT = sbt(st, "rawT", [128, T + 8], BF16); xcT = sbt(st, "xcT", [128, T], BF16)
                diag = sbt(st, "diag", [128, 5, 128], BF16)
                cbS = [sbt(st, "cbS%d" % d, [128, 128]) for d in range(2)]
                Et = [sbt(st, "Et%d" % d, [128, 128]) for d in range(3)]
                Wt = [sbt(st, "Wt%d" % d, [128, 128], BF16) for d in range(3)]
                H32 = [sbt(st, "H32_%d" % d, [128, 256]) for d in range(2)]
                Hbf = [sbt(st, "Hbf_%d" % d, [128, 256], BF16) for d in range(2)]
                xs_ = [sbt(st, "xs%d" % d, [128, 256], BF16) for d in range(2)]
                tmp1 = [sbt(st, "tmp1_%d" % d, [128, 256]) for d in range(2)]
                sgt = [sbt(st, "sgt1_%d" % d, [128, 512]) for d in range(2)]
                u32 = [sbt(st, "u32_%d" % d, [128, 512]) for d in range(2)]
                t1 = [sbt(st, "t1_%d" % d, [128, 256]) for d in range(2)]
                ygt = [sbt(st, "ygt1_%d" % d, [128, 256], BF16) for d in range(2)]
                junk = sbt(st, "junkB1", [128, 256], BF16)
                ss_all = sbt(st, "ss_all1", [128, NT, 16])
                K.memset("pool", ss_all[:], 1.0, w=["ss_all"])
                K.memset("pool", rawT[:], 0.0, w=["rawT"])

                def rawcol(tok0):
                    return tok0 + 2 if tok0 < CT else tok0 + 6

                sil_i = [0]

                def proj_conv(slot, col0, cwidx, dest, dkey, need_ctx):
                    wk = "Wr%d" % slot
                    K.dma("pool", Wr[:, :, slot * 128:(slot + 1) * 128], wv[:, :, col0:col0 + 128], w=[wk])
                    grp = GROUPS if need_ctx else GROUPS[1:]
                    for (g, tok0, n) in grp:
                        pf = bank[g % 2][:, 0:n]; pfk = "bk%d" % (g % 2)
                        for kc in range(KC):
                            K.mm(pf, Wr[:, kc, slot * 128:(slot + 1) * 128], hT[:, kc, tok0:tok0 + n], start=(kc == 0), stop=(kc == KC - 1),
                                 r=[wk], w=[pfk], acc=(kc > 0))
                        rc0 = rawcol(tok0)
                        K.cp("act", rawT[:, rc0:rc0 + n], pf, r=[pfk], w=["raw%d" % g])
                    for j in range(5):
                        K.act(diag[:, j, :], ident[:], AF.Identity, scale=cwT[:, cwidx, j:j + 1], r=["ident", "cwT"], w=["diag"])
                    for (g, tok0, n) in grp:
                        pc = bank[2 + g % 2][:, 0:n]; pck = "bk%d" % (2 + g % 2)
                        rc0 = rawcol(tok0)
                        rk_ = ["raw%d" % gg for gg in range(5)]
                        for j in range(5):
                            K.mm(pc, diag[:, j, :], rawT[:, rc0 + j - 2:rc0 + j - 2 + n], start=(j == 0), stop=(j == 4),
                                 r=["diag"] + rk_, w=[pck], acc=(j > 0))
                        i_ = sil_i[0] % 2; sil_i[0] += 1
                        e_ = sgt[i_][:, 0:n]; ek = "sgt%d" % i_
                        u_ = u32[i_][:, 0:n]; uk = "u32_%d" % i_
                        K.act(u_, pc, AF.Identity, bias=cbT[:, cwidx:cwidx + 1], r=[pck, "cbT"], w=[uk])
                        K.act(e_, u_, AF.Exp, scale=-1.0, r=[uk], w=[ek])
                        K.ts("dve", e_, e_, 1.0, None, ALU.add, r=[ek], w=[ek])
                        K.recip(e_, e_, r=[ek], w=[ek])
                        K.tt("pool", dest[:, tok0:tok0 + n], u_, e_, ALU.mult, r=[uk, ek], w=[dkey + "_%d" % g])

                def to_tm(src, skey, dst3, c0col, dkey, need_ctx):
                    for t in (range(NT) if need_ctx else range(2, NT)):
                        g = tile_group(t)
                        pb_ = pbfs[t % 2][:, 0:128]; pbk = "pb%d" % (t % 2)
                        K.tr(pb_, src[:, t * 128:(t + 1) * 128], ident[:], r=[skey + "_%d" % g, "ident"], w=[pbk])
                        K.cp("act" if t % 2 else "dve", dst3[:, t, c0col:c0col + 128], pb_, r=[pbk], w=[dkey + "_%d" % t])

                fwc = list(range(NT)); bwc = [1, 0] + list(range(NT - 1, 1, -1))
                for g in range(int(_os.environ.get("NGRP1", "8"))):
                    proj_conv(0, 8192 + g * 128, 32 + g, BT, "BT", True)
                    to_tm(BT, "BT", B_tm, 0, "B_tm", True)
                    proj_conv(1, 9216 + g * 128, 40 + g, CTt, "CT", False)
                    for half in range(2):
                        u = 2 * g + half
                        hd0 = g * 8 + half * 4
                        for b in range(2):
                            xb = g * 4 + half * 2 + b
                            proj_conv(b, 4096 + xb * 128, xb, xcT, "xcT", True)
                            to_tm(xcT, "xcT", x_tm, b * 128, "x_tm%d" % b, True)
                        xk_all = ["x_tm%d_%d" % (b, t) for b in range(2) for t in range(NT)]
                        for b in range(2):
                            zc = (g * 4 + half * 2 + b) * 128
                            K.dma("pool", Wr[:, :, 256 + b * 128:256 + (b + 1) * 128], wv[:, :, zc:zc + 128], w=["Wr%d" % (2 + b)])
                        for t in range(2, NT):
                            pt = bank[2 + t % 4][:, 0:256]; ptk = "bk%d" % (2 + t % 4)
                            for kc in range(KC):
                                K.mm(pt, hT[:, kc, t * 128:(t + 1) * 128], Wr[:, kc, 256:512], start=(kc == 0), stop=(kc == KC - 1),
                                     r=["Wr2", "Wr3"], w=[ptk], acc=(kc > 0))
                            i_ = sil_i[0] % 2; sil_i[0] += 1
                            e_ = sgt[i_][:, 0:256]; ek = "sgt%d" % i_
                            u_ = u32[i_][:, 0:256]; uk = "u32_%d" % i_
                            K.cp("act", u_, pt, r=[ptk], w=[uk])
                            K.act(e_, u_, AF.Exp, scale=-1.0, r=[uk], w=[ek])
                            K.ts("dve", e_, e_, 1.0, None, ALU.add, r=[ek], w=[ek])
                            K.recip(e_, e_, r=[ek], w=[ek])
                            K.tt("pool", sz[:, t - 2, :], u_, e_, ALU.mult, r=[uk, ek], w=["sz%d" % t])
                        for d in range(2):
                            K.memset("pool", H32[d][:], 0.0, w=["H32_%d" % d])
                            K.memset("pool", Hbf[d][:], 0.0, w=["Hbf_%d" % d])
                        visited = set()
                        for step in range(NT):
                            for d in range(2):
                                c = fwc[step] if d == 0 else bwc[step]
                                r0 = d * 64 + hd0
                                xk = ["x_tm%d_%d" % (b, c) for b in range(2)]
                                if c >= 2:
                                    yi = bank[0 + d][:, 0:256]; yik = "bk%d" % d
                                    K.mm(yi, CTt[:, c * 128:(c + 1) * 128], Hbf[d][:], r=["CT_%d" % tile_group(c), "Hbf_%d" % d], w=[yik])
                                    scb = sOutTM[:, c, r0:r0 + 4].unsqueeze(2).to_broadcast([128, 4, 64])
                                    yi3 = yi.rearrange("p (h e) -> p h e", e=64)
                                    if c not in visited:
                                        visited.add(c)
                                        K.tt("dve", Yint[:, c - 2, :].rearrange("p (h e) -> p h e", e=64), yi3, scb, ALU.mult,
                                             r=[yik, "sOutTM"], w=["Yint%d" % c])
                                    else:
                                        K.tt("dve", tmp1[d][:].rearrange("p (h e) -> p h e", e=64), yi3, scb, ALU.mult,
                                             r=[yik, "sOutTM"], w=["tmp1_%d" % d])
                                        K.tt("dve", Yint[:, c - 2, :], Yint[:, c - 2, :], tmp1[d][:], ALU.add,
                                             r=["Yint%d" % c, "tmp1_%d" % d], w=["Yint%d" % c])
                                if step < NT - 1:
                                    wsb = wSTM[:, c, r0:r0 + 4].unsqueeze(2).to_broadcast([128, 4, 64])
                                    K.tt("pool", xs_[d][:].rearrange("p (h e) -> p h e", e=64), x_tm[:, c, :].rearrange("p (h e) -> p h e", e=64),
                                         wsb, ALU.mult, r=xk + ["wSTM"], w=["xs%d" % d])
                                    pu = bank[2 + d][:, 0:256]; puk = "bk%d" % (2 + d)
                                    K.mm(pu, B_tm[:, c, :], xs_[d][:], r=["B_tm_%d" % c, "xs%d" % d], w=[puk])
                                    acb = abc[:, c, r0:r0 + 4].unsqueeze(2).to_broadcast([128, 4, 64])
                                    h3 = H32[d][:].rearrange("p (h e) -> p h e", e=64)
                                    K.tt("dve", h3, h3, acb, ALU.mult, r=["H32_%d" % d, "abc"], w=["H32_%d" % d])
                                    K.tt("dve", H32[d][:], H32[d][:], pu, ALU.add, r=["H32_%d" % d, puk], w=["H32_%d" % d])
                                    K.cp("act", Hbf[d][:], H32[d][:], r=["H32_%d" % d], w=["Hbf_%d" % d])
                        its = [(c, j, d) for c in range(2, NT) for j in range(4) for d in range(2)]
                        NI = len(its)

                        def p2_prologue(c):
                            cs = slice(c * 128, (c + 1) * 128)
                            pcb = bank[2 + c % 2][:, 0:128]; pck = "bk%d" % (2 + c % 2)
                            K.mm(pcb, BT[:, cs], CTt[:, cs], r=["BT_%d" % tile_group(c), "CT_%d" % tile_group(c)], w=[pck])
                            K.cp("act", cbS[c % 2][:], pcb, r=[pck], w=["cbS%d" % (c % 2)])

                        def p2_sel(i):
                            c, j, d = its[i]
                            cs = slice(c * 128, (c + 1) * 128)
                            r = d * 64 + hd0 + j
                            dps = bank[d][:, 0:128]; dpk = "bk%d" % d
                            jr = hd0 + j
                            sel = id2[:, jr:jr + 1].to_broadcast([128, 128])
                            K.mm(dps, sel, QQ[d][:, cs], start=True, stop=False, r=["id2"], w=[dpk])
                            K.mm(dps, ident[:], (negF if d == 0 else negB)[:], start=False, stop=True, r=["ident", "negF", "negB"], w=[dpk], acc=True)

                        def p2_exp(i):
                            c, j, d = its[i]
                            r = d * 64 + hd0 + j
                            k3 = i % 3
                            K.act(Et[k3][:], bank[d][:, 0:128], AF.Exp, bias=biasTM[:, c, r:r + 1], scale=(1.0 if d == 0 else -1.0),
                                  r=["bk%d" % d, "biasTM"], w=["Et%d" % k3])
                            K.tt("dve", Wt[k3][:], Et[k3][:], cbS[c % 2][:], ALU.mult, r=["Et%d" % k3, "cbS%d" % (c % 2)], w=["Wt%d" % k3])

                        def p2_y(i):
                            c, j, d = its[i]
                            k3 = i % 3
                            yps = bank[4 + c % 2][:, 0:256]
                            xk = ["x_tm%d_%d" % (b, c) for b in range(2)]
                            K.mm(yps[:, j * 64:(j + 1) * 64], Wt[k3][:], x_tm[:, c, j * 64:(j + 1) * 64], start=(d == 0), stop=(d == 1),
                                 r=["Wt%d" % k3] + xk, w=["bk%d" % (4 + c % 2)], acc=(not (j == 0 and d == 0)))

                        def p2_final(c):
                            i2 = c % 2
                            yps = bank[4 + c % 2][:, 0:256]
                            xk = ["x_tm%d_%d" % (b, c) for b in range(2)]
                            t3 = t1[i2][:].rearrange("p (h e) -> p h e", e=64)
                            K.tt("pool", t3, x_tm[:, c, :].rearrange("p (h e) -> p h e", e=64),
                                 Dbc[:, hd0:hd0 + 4].unsqueeze(2).to_broadcast([128, 4, 64]), ALU.mult, r=xk + ["Dbc"], w=["t1_%d" % i2])
                            K.tt("pool", t1[i2][:], t1[i2][:], Yint[:, c - 2, :], ALU.add, r=["t1_%d" % i2, "Yint%d" % c], w=["t1_%d" % i2])
                            K.tt("dve", t1[i2][:], t1[i2][:], yps, ALU.add, r=["t1_%d" % i2, "bk%d" % (4 + c % 2)], w=["t1_%d" % i2])
                            K.tt("pool", ygt[i2][:], t1[i2][:], sz[:, c - 2, :], ALU.mult, r=["t1_%d" % i2, "sz%d" % c], w=["ygt%d" % i2])
                            K.act(junk[:], ygt[i2][:], AF.Square, accum=ss_all[:, c, u:u + 1], r=["ygt%d" % i2], w=["junkB1", "ss_all"])
                            K.dma("sp", yg_d[c * 128:(c + 1) * 128, u * 256:(u + 1) * 256], ygt[i2][:], r=["ygt%d" % i2])

                        for i in range(NI + 2):
                            if i < NI:
                                c, j, d = its[i]
                                if j == 0 and d == 0:
                                    p2_prologue(c)
                                p2_sel(i)
                            if 1 <= i <= NI:
                                p2_exp(i - 1)
                            if i >= 2:
                                p2_y(i - 2)
                                c2, j2, d2 = its[i - 2]
                                if j2 == 3 and d2 == 1:
                                    p2_final(c2)
                finish_ss(st, ss_all, NT, float(DI))
                K.dma("sp", rs_d.rearrange("(c p) -> p c", p=128), rstdo[:, 2:NT], r=["rstdo"], w=["rs_d"], slow=True)
                for hb in range(2):
                    K.dma("sp", rstdo[hb * 64:(hb + 1) * 64, 2:NT], rs_d.rearrange("(col t two) -> two col t", two=2, t=16)[hb],
                          r=["rs_d"], w=["rstdo"], slow=True)
            P.barrier()
        if stop_after == "b1":
            P.emit(); return nc
        with ExitStack() as st:
            phaseC(st, 1, b_out_w_d, x1_d, ctx1_d, "rstdo")
        P.barrier()

        P.emit()
    return nc


_CACHE = {}


def kernel(x, c, ctx, c_ctx, norm_w, mod_w, mod_b, a_in_w, a_lb, a_onorm_w, a_out_w,
           b_in_w, b_conv_w, b_conv_b, b_dt_bias, b_a_log, b_d, b_onorm_w, b_out_w, final_norm_w):
    f = lambda a: np.ascontiguousarray(np.asarray(a, dtype=np.float32))
    shared = {
        "c_ctx": f(c_ctx), "norm_w": f(norm_w), "mod_w": f(mod_w), "mod_b": f(mod_b),
        "a_in_w": f(a_in_w)[0], "a_lb": f(a_lb), "a_onorm_w": f(a_onorm_w)[0], "a_out_w": f(a_out_w)[0],
        "b_in_w": f(b_in_w)[0], "b_conv_w": f(b_conv_w)[0], "b_conv_b": f(b_conv_b)[0],
        "b_dt_bias": f(b_dt_bias).reshape(128), "b_a_log": f(b_a_log).reshape(128), "b_d": f(b_d)[0],
        "b_onorm_w": f(b_onorm_w)[0], "b_out_w": f(b_out_w)[0], "final_norm_w": f(final_norm_w),
    }
    x = f(x); c = f(c); ctx = f(ctx)
    if "nc" not in _CACHE:
        _CACHE["nc"] = build_nc()
    nc = _CACHE["nc"]
    in_maps = []
    for b in range(8):
        m = dict(shared)
        m["x"] = x[b]; m["c"] = c[b]; m["ctx"] = ctx[b]
        in_maps.append(m)
    res = run_bass_kernel_spmd(nc, in_maps, core_ids=list(range(8)))
    return np.stack([np.asarray(r["out"], dtype=np.float32) for r in res.results], axis=0)
```

```python
import os as _os
import numpy as np
from contextlib import ExitStack
import concourse.bass as bass
import concourse.mybir as mybir
from concourse.bass_utils import run_bass_kernel_spmd

F32 = mybir.dt.float32
BF16 = mybir.dt.bfloat16
AF = mybir.ActivationFunctionType
ALU = mybir.AluOpType

D = 2048
L = 2048
CT = 256
T = L + CT
KC = 16
DI = 4096
EPS = 1e-6
NT = T // 128
ENGS = ("pe", "act", "dve", "pool", "sp")
NSLOT = 8


class Prog:
    def __init__(self, nc, stack):
        self.nc = nc
        self.stack = stack
        self.ops = []
        self.last_w = {}
        self.readers = {}
        self.bar = set()
        self.last_eng = {}
        self.dma_recent = {e: [] for e in ENGS}

    def op(self, eng, fn, reads=(), writes=(), acc=False, dma=False):
        if getattr(self, "cap_active", False):
            self.capcount = getattr(self, "capcount", 0) + 1
            if self.capcount > self.cap:
                return None
        i = len(self.ops)
        pk_ = [k for k in reads if k[:2] in ("bk", "pb") and k not in writes]
        if pk_:
            writes = list(writes) + pk_
        deps = set(self.bar)
        for k in reads:
            if k in self.last_w:
                deps.add(self.last_w[k])
        for k in writes:
            if k in self.last_w:
                j = self.last_w[k]
                if not (acc and self.ops[j]["eng"] == eng and not self.ops[j]["dma"]):
                    deps.add(j)
            for j in self.readers.get(k, ()):
                deps.add(j)
        self.ops.append(dict(eng=eng, fn=fn, deps=deps, dma=dma, sig=False))
        for k in reads:
            self.readers.setdefault(k, []).append(i)
        for k in writes:
            self.last_w[k] = i
            self.readers[k] = []
        if dma:
            self.dma_recent[eng].append(i)
            self.dma_recent[eng] = self.dma_recent[eng][-NSLOT:]
        else:
            self.last_eng[eng] = i
        return i

    def barrier(self):
        b = set(self.last_eng.values())
        for e in ENGS:
            b.update(self.dma_recent[e])
        self.bar = b
        self.last_w = {}
        self.readers = {}

    def emit(self):
        nc = self.nc
        ops = self.ops
        for o in ops:
            for d in o["deps"]:
                ops[d]["sig"] = True
        sem_c = {e: self.stack.enter_context(nc.semaphore("c_" + e)) for e in ENGS if e != "sp"}
        sem_d = {e: [self.stack.enter_context(nc.semaphore("d_%s_%d" % (e, s))) for s in range(NSLOT)]
                 for e in ("sp", "act", "pool")}
        cnt = {e: 0 for e in ENGS}
        dcnt = {e: 0 for e in ENGS}
        for o in ops:
            e = o["eng"]
            if o["dma"]:
                n = dcnt[e]
                dcnt[e] += 1
                o["semh"] = sem_d[e][n % NSLOT]
                o["semkey"] = ("d", e, n % NSLOT)
                o["val"] = 16 * (n // NSLOT + 1)
                o["dn"] = n
            elif o["sig"]:
                cnt[e] += 1
                o["semh"] = sem_c[e]
                o["semkey"] = ("c", e)
                o["val"] = cnt[e]
        known = {e: {} for e in ENGS}
        streams = {e: [] for e in ENGS}
        dma_hist = {e: [] for e in ENGS}
        for o in ops:
            e = o["eng"]
            waits = {}
            for d in o["deps"]:
                od = ops[d]
                k = od["semkey"]
                if od["val"] > waits.get(k, (None, 0))[1]:
                    waits[k] = (od["semh"], od["val"])
            if o["dma"]:
                n = o["dn"]
                if n >= NSLOT:
                    prev = dma_hist[e][n - NSLOT]
                    k = prev["semkey"]
                    if prev["val"] > waits.get(k, (None, 0))[1]:
                        waits[k] = (prev["semh"], prev["val"])
                dma_hist[e].append(o)
            wl = []
            for k, (h, v) in waits.items():
                if known[e].get(k, 0) >= v:
                    continue
                known[e][k] = v
                wl.append((h, v))
            streams[e].append((o, wl))
        fin = []
        for e in ("sp", "act", "pool"):
            for o in dma_hist[e][-NSLOT:]:
                fin.append((o["semh"], o["val"]))
        self.n_instr = {e: len(streams[e]) for e in ENGS}
        self.sem_max = dict(cnt); self.dma_cnt = dict(dcnt)

        def run(eng_name):
            def body(engine):
                for o, wl in streams[eng_name]:
                    for h, v in wl:
                        engine.wait_ge(h, v)
                    ins = o["fn"](engine)
                    if o["dma"]:
                        ins.then_inc(o["semh"], 16)
                    elif o["sig"]:
                        ins.then_inc(o["semh"], 1)
                if eng_name == "sp":
                    for h, v in fin:
                        engine.wait_ge(h, v)
            return body

        with nc.Block() as block:
            block.tensor(run("pe"))
            block.scalar(run("act"))
            block.vector(run("dve"))
            block.gpsimd(run("pool"))
            block.sync(run("sp"))


class KB:
    def __init__(self, P):
        self.P = P

    def mm(self, out, lhsT, rhs, start=True, stop=True, r=(), w=(), acc=False):
        self.P.op("pe", lambda e: e.matmul(out, lhsT=lhsT, rhs=rhs, start=start, stop=stop), r, w, acc=acc)

    def tr(self, out, in_, ident, r=(), w=(), acc=False):
        self.P.op("pe", lambda e: e.transpose(out, in_, ident), r, w, acc=acc)

    def act(self, out, in_, func, bias=None, scale=None, accum=None, r=(), w=()):
        kw = {}
        if bias is not None:
            kw["bias"] = bias
        if scale is not None:
            kw["scale"] = scale
        if accum is not None:
            kw["accum_out"] = accum
        self.P.op("act", lambda e: e.activation(out=out, in_=in_, func=func, **kw), r, w)

    def tt(self, eng, out, in0, in1, op, r=(), w=()):
        self.P.op(eng, lambda e: e.tensor_tensor(out=out, in0=in0, in1=in1, op=op), r, w)

    def ts(self, eng, out, in0, s1, s2, op0, op1=None, r=(), w=()):
        if op1 is None:
            self.P.op(eng, lambda e: e.tensor_scalar(out=out, in0=in0, scalar1=s1, scalar2=None, op0=op0), r, w)
        else:
            self.P.op(eng, lambda e: e.tensor_scalar(out=out, in0=in0, scalar1=s1, scalar2=s2, op0=op0, op1=op1), r, w)

    def stt(self, eng, out, in0, scalar, in1, op0, op1, r=(), w=()):
        self.P.op(eng, lambda e: e.scalar_tensor_tensor(out=out, in0=in0, scalar=scalar, in1=in1, op0=op0, op1=op1), r, w)

    def cp(self, eng, out, in_, r=(), w=()):
        if eng == "act":
            self.P.op("act", lambda e: e.copy(out=out, in_=in_), r, w)
        else:
            self.P.op(eng, lambda e: e.tensor_copy(out=out, in_=in_), r, w)

    def memset(self, eng, out, val, w=()):
        self.P.op(eng, lambda e: e.memset(out, val), (), w)

    def scan(self, out, d0, d1, r=(), w=()):
        self.P.op("dve", lambda e: e.tensor_tensor_scan(out=out, data0=d0, data1=d1, initial=0.0,
                                                        op0=ALU.mult, op1=ALU.add), r, w)

    def recip(self, out, in_, r=(), w=()):
        self.P.op("dve", lambda e: e.reciprocal(out=out, in_=in_), r, w)

    def asel(self, out, in_, cmp, fill, base, pattern, cm, r=(), w=()):
        self.P.op("pool", lambda e: e.affine_select(out=out, in_=in_, compare_op=cmp, fill=fill, base=base,
                                                    pattern=pattern, channel_multiplier=cm), r, w)

    def dma(self, q, out, in_, r=(), w=(), slow=False):
        if slow:
            self.P.op(q, lambda e: e.dma_start(out=out, in_=in_, allow_slow_non_contiguous=True), r, w, dma=True)
        else:
            self.P.op(q, lambda e: e.dma_start(out=out, in_=in_), r, w, dma=True)


def build_nc(stop_after=None, debug=False):
    nc = bass.Bass("TRN2", target_bir_lowering=False)

    def din(name, shape):
        return nc.dram_tensor(name, shape, F32, kind="ExternalInput").ap()

    x_d = din("x", [L, D]); ctx_d = din("ctx", [CT, D]); c_d = din("c", [D]); cc_d = din("c_ctx", [D])
    norm_w_d = din("norm_w", [2, D]); mod_w_d = din("mod_w", [2, D, 3 * D]); mod_b_d = din("mod_b", [2, 3 * D])
    a_in_w_d = din("a_in_w", [D, 14336]); a_lb_d = din("a_lb", [2, 2, 2048]); a_on_d = din("a_onorm_w", [DI])
    a_out_w_d = din("a_out_w", [DI, D]); b_in_w_d = din("b_in_w", [D, 10368]); b_cw_d = din("b_conv_w", [5, 6144])
    b_cb_d = din("b_conv_b", [6144]); b_dtb_d = din("b_dt_bias", [128]); b_alog_d = din("b_a_log", [128])
    b_d_d = din("b_d", [64]); b_on_d = din("b_onorm_w", [DI]); b_out_w_d = din("b_out_w", [DI, D])
    fnw_d = din("final_norm_w", [D])
    out_d = nc.dram_tensor("out", [L, D], F32, kind="ExternalOutput").ap()
    x1_d = nc.dram_tensor("x1s", [L, D], F32, kind="ExternalOutput" if debug else "Internal").ap()
    ctx1_d = nc.dram_tensor("ctx1s", [CT, D], F32, kind="ExternalOutput" if debug else "Internal").ap()
    yg_d = nc.dram_tensor("ygs", [T, DI], BF16, kind="Internal").ap()
    gt_d = nc.dram_tensor("gts", [3, D], F32, kind="Internal").ap()
    rs_d = nc.dram_tensor("rss", [L], F32, kind="Internal").ap()

    with ExitStack() as st0:
        P = Prog(nc, st0)
        K = KB(P)
        global _LASTP
        _LASTP = P

        _uid = [0]

        def sbt(st, name, shape, dt=F32):
            _uid[0] += 1
            return st.enter_context(nc.sbuf_tensor("%s_u%d" % (name, _uid[0]), shape, dt))

        bank = [st0.enter_context(nc.psum_tensor("bank%d" % i, [128, 512], F32)) for i in range(6)]
        pbfs = [st0.enter_context(nc.psum_tensor("pbf%d" % i, [128, 1024], BF16)) for i in range(2)]

        ident32 = sbt(st0, "ident32", [128, 128]); ident = sbt(st0, "ident", [128, 128], BF16)
        maskF = sbt(st0, "maskF", [128, 128]); maskB = sbt(st0, "maskB", [128, 128])
        negF = sbt(st0, "negF", [128, 128], BF16); negB = sbt(st0, "negB", [128, 128], BF16)
        ones32 = sbt(st0, "ones32", [128, 128])
        lbT = sbt(st0, "lbT", [128, 32])
        modT = sbt(st0, "modT", [128, 2, 32, 2])
        Amod = sbt(st0, "Amod", [128, 2, 2, 16]); SHmod = sbt(st0, "SHmod", [128, 2, 2, 16])
        onT = sbt(st0, "onT", [128, 2, 32])
        cwT = sbt(st0, "cwT", [128, 48, 5]); cbT = sbt(st0, "cbT", [128, 48])
        dtb = sbt(st0, "dtb", [128, 1]); aneg = sbt(st0, "aneg", [128, 1])
        Dbc = sbt(st0, "Dbc", [128, 64])
        rstdo = sbt(st0, "rstdo", [128, NT])

        K.memset("pool", ident32[:], 0.0, w=["ident32"])
        K.asel(ident32[:], ident32[:], ALU.not_equal, 1.0, 0, [[-1, 128]], 1, r=["ident32"], w=["ident32"])
        K.cp("dve", ident[:], ident32[:], r=["ident32"], w=["ident"])
        K.memset("dve", ones32[:], 1.0, w=["ones32"])
        K.memset("pool", maskF[:], 1.0, w=["maskF"])
        K.asel(maskF[:], maskF[:], ALU.is_ge, 0.0, 0, [[1, 128]], -1, r=["maskF"], w=["maskF"])
        K.memset("pool", maskB[:], 1.0, w=["maskB"])
        K.asel(maskB[:], maskB[:], ALU.is_ge, 0.0, 0, [[-1, 128]], 1, r=["maskB"], w=["maskB"])
        K.memset("pool", negF[:], 0.0, w=["negF"])
        K.asel(negF[:], negF[:], ALU.is_ge, -30000.0, 0, [[1, 128]], -1, r=["negF"], w=["negF"])
        K.memset("pool", negB[:], 0.0, w=["negB"])
        K.asel(negB[:], negB[:], ALU.is_ge, 30000.0, 0, [[-1, 128]], 1, r=["negB"], w=["negB"])

        with ExitStack() as st:
            c2 = sbt(st, "c2", [128, 16, 2]); e2 = sbt(st, "e2", [128, 16, 2]); s2 = sbt(st, "s2", [128, 16, 2])
            lb0 = sbt(st, "lb0", [128, 32]); lb1 = sbt(st, "lb1", [128, 32])
            nwT = sbt(st, "nwT", [128, 2, 16]); modb = sbt(st, "modb", [1, 2, 3 * D])
            alg = sbt(st, "alg", [128, 1]); gtrow = sbt(st, "gtrow", [128, 512])
            slab = [sbt(st, "slab%d" % i, [128, 16, 512]) for i in range(2)]
            stage = sbt(st, "stage", [128, 128])
            ld_cnt = [0]

            def load_cm(dst_ap, src_rows_ap, J, dkey):
                i_ = ld_cnt[0]; ld_cnt[0] += 1
                K.dma("sp", stage[0:J, :], src_rows_ap, w=["stage"])
                pz = bank[5][:, 0:J]
                K.tr(pz, stage[0:J, :], ident32[0:J, 0:J], r=["stage", "ident32"], w=["bk5"])
                K.cp("dve", dst_ap, pz, r=["bk5"], w=[dkey])

            load_cm(c2[:, :, 0], c_d.rearrange("(j p) -> j p", p=128), 16, "c2")
            load_cm(c2[:, :, 1], cc_d.rearrange("(j p) -> j p", p=128), 16, "c2")
            load_cm(lb0[:], a_lb_d[0].rearrange("d (h p) -> (d h) p", p=128), 32, "lb0")
            load_cm(lb1[:], a_lb_d[1].rearrange("d (h p) -> (d h) p", p=128), 32, "lb1")
            load_cm(nwT[:].rearrange("p l j -> p (l j)"), norm_w_d.rearrange("l (j p) -> (l j) p", p=128), 32, "nwT")
            K.dma("sp", modb[:], mod_b_d.rearrange("(o l) c -> o l c", o=1), w=["modb"])
            load_cm(onT[:, 0, :], a_on_d.rearrange("(j p) -> j p", p=128), 32, "onT")
            load_cm(onT[:, 1, :], b_on_d.rearrange("(j p) -> j p", p=128), 32, "onT")
            for k_ in range(5):
                load_cm(cwT[:, :, k_], b_cw_d[k_].rearrange("(j p) -> j p", p=128), 48, "cwT")
            load_cm(cbT[:], b_cb_d.rearrange("(j p) -> j p", p=128), 48, "cbT")
            K.dma("sp", dtb[:], b_dtb_d.rearrange("(p o) -> p o", o=1), w=["dtb"])
            K.dma("sp", alg[:], b_alog_d.rearrange("(p o) -> p o", o=1), w=["alg"])
            K.dma("sp", Dbc[:], b_d_d.partition_broadcast(128), w=["Dbc"])
            K.act(aneg[:], alg[:], AF.Exp, r=["alg"], w=["aneg"])
            K.ts("dve", aneg[:], aneg[:], -1.0, None, ALU.mult, r=["aneg"], w=["aneg"])
            K.tt("dve", lb0[:], lb0[:], lb1[:], ALU.subtract, r=["lb0", "lb1"], w=["lb0"])
            K.act(lb1[:], lb0[:], AF.Exp, scale=-1.0, r=["lb0"], w=["lb1"])
            K.ts("dve", lb1[:], lb1[:], 1.0, None, ALU.add, r=["lb1"], w=["lb1"])
            K.recip(lbT[:], lb1[:], r=["lb1"], w=["lbT"])
            K.act(e2[:], c2[:], AF.Exp, scale=-1.0, r=["c2"], w=["e2"])
            K.ts("dve", e2[:], e2[:], 1.0, None, ALU.add, r=["e2"], w=["e2"])
            K.recip(e2[:], e2[:], r=["e2"], w=["e2"])
            K.tt("dve", s2[:], c2[:], e2[:], ALU.mult, r=["c2", "e2"], w=["s2"])
            pm = bank[0]
            for li in range(2):
                mw = mod_w_d[li].rearrange("(kc p) c -> p kc c", p=128)
                for s in range(12):
                    sl = slab[s % 2]; sk = "slab%d" % (s % 2)
                    K.dma("sp", sl[:], mw[:, :, s * 512:(s + 1) * 512], w=[sk])
                    if s < 8:
                        for j in range(4):
                            ch = s * 4 + j
                            po = pm[:, ch * 2:ch * 2 + 2]
                            for kc in range(KC):
                                K.mm(po, sl[:, kc, j * 128:(j + 1) * 128], s2[:, kc, :], start=(kc == 0), stop=False,
                                     r=[sk, "s2"], w=["bk0"], acc=(kc > 0))
                            K.mm(po, modb[0:1, li, ch * 128:(ch + 1) * 128], ones32[0:1, 0:2], start=False, stop=True,
                                 r=["modb", "ones32"], w=["bk0"], acc=True)
                        if s == 7:
                            K.cp("dve", modT[:, li, :, :], pm[:, 0:64].rearrange("p (c m) -> p c m", m=2), r=["bk0"], w=["modT"])
                    else:
                        nm = 2 if li == 0 else 1
                        for m in range(nm):
                            pg = bank[1 + m]; pk = "bk%d" % (1 + m)
                            for kc in range(KC):
                                K.mm(pg[:], s2[:, kc, m:m + 1].to_broadcast([128, 128]), sl[:, kc, :], start=(kc == 0), stop=False,
                                     r=[sk, "s2"], w=[pk], acc=(kc > 0))
                            K.mm(pg[:], ones32[0:1, 0:128], modb[0:1, li, s * 512:(s + 1) * 512], start=False, stop=True,
                                 r=["modb", "ones32"], w=[pk], acc=True)
                            K.cp("act", gtrow[:], pg[:], r=[pk], w=["gtrow"])
                            row = (0 if m == 0 else 1) if li == 0 else 2
                            K.dma("sp", gt_d[row:row + 1, (s - 8) * 512:(s - 7) * 512], gtrow[0:1, :], r=["gtrow"])
                for m in range(2):
                    K.stt("dve", Amod[:, li, m, :], modT[:, li, 16:32, m], 1.0, nwT[:, li, :], ALU.add, ALU.mult,
                          r=["modT", "nwT"], w=["Amod"])
                    K.cp("dve", SHmod[:, li, m, :], modT[:, li, 0:16, m], r=["modT"], w=["SHmod"])
        P.barrier()
        if stop_after == "p0":
            P.emit(); return nc

        def phaseA(st, li, hT, xsrc_d, csrc_d):
            xin = [sbt(st, "xin%d" % i, [128, D]) for i in range(3)]
            xn = [sbt(st, "xn%d" % i, [128, 4, D], BF16) for i in range(2)]
            junk = sbt(st, "junkA", [128, D], BF16)
            ssA = sbt(st, "ssA", [128, NT]); lnA = sbt(st, "lnA", [128, NT]); rsA = sbt(st, "rsA", [128, NT])
            groups = [(0, 2)] + [(1 + g, 4) for g in range(4)]
            tile_i = 0
            for gi, (g, ntile) in enumerate(groups):
                xg = xn[gi % 2]; xgk = "xn%d" % (gi % 2)
                m = 1 if g == 0 else 0
                for tl in range(ntile):
                    src = csrc_d[tl * 128:(tl + 1) * 128, :] if g == 0 else xsrc_d[((g - 1) * 4 + tl) * 128:((g - 1) * 4 + tl + 1) * 128, :]
                    xb = xin[tile_i % 3]; xk = "xin%d" % (tile_i % 3)
                    col = tile_i
                    K.dma("sp", xb[:], src, w=[xk])
                    K.act(junk[:], xb[:], AF.Square, accum=ssA[:, col:col + 1], r=[xk], w=["junkA", "ssA%d" % col])
                    K.act(lnA[:, col:col + 1], ssA[:, col:col + 1], AF.Ln, bias=EPS, scale=1.0 / D, r=["ssA%d" % col], w=["lnA%d" % col])
                    K.act(rsA[:, col:col + 1], lnA[:, col:col + 1], AF.Exp, scale=-0.5, r=["lnA%d" % col], w=["rsA%d" % col])
                    K.ts("pool" if tile_i % 2 else "dve", xg[:, tl, :], xb[:], rsA[:, col:col + 1], None, ALU.mult,
                         r=[xk, "rsA%d" % col], w=[xgk + "_%d" % tl])
                    tile_i += 1
                n = ntile * 128
                tok0 = 0 if g == 0 else CT + (g - 1) * 512
                for dc in range(KC):
                    pbf = pbfs[dc % 2]
                    pt = pbf[:, 0:n]; pk = "pb%d" % (dc % 2)
                    for tl in range(ntile):
                        K.tr(pbf[:, tl * 128:(tl + 1) * 128], xg[:, tl, dc * 128:(dc + 1) * 128], ident[:],
                             r=[xgk + "_%d" % tl, "ident"], w=[pk], acc=(tl > 0))
                    if li == 1 and g > 0:
                        o_ap = hT[:, dc, CT:T].rearrange("p (c r) -> p c r", r=32)[:, :, (g - 1) * 8:g * 8]
                        i_ap = pt.rearrange("p (r c) -> p c r", c=64)
                    else:
                        o_ap = hT[:, dc, tok0:tok0 + n]
                        i_ap = pt
                    hk = "hT%d_%d" % (dc, g)
                    if dc % 2 == 0:
                        K.act(o_ap, i_ap, AF.Identity, bias=SHmod[:, li, m, dc:dc + 1], scale=Amod[:, li, m, dc:dc + 1],
                              r=[pk, "Amod", "SHmod"], w=[hk])
                    else:
                        K.ts("dve", o_ap, i_ap, Amod[:, li, m, dc:dc + 1], SHmod[:, li, m, dc:dc + 1], ALU.mult, ALU.add,
                             r=[pk, "Amod", "SHmod"], w=[hk])

        GROUPS = [(0, 0, 256)] + [(1 + g, CT + g * 512, 512) for g in range(4)]

        def tile_group(t):
            return 0 if t < 2 else 1 + (t - 2) // 4

        def phaseC(st, li, ow_d, xsrc_d, csrc_d, ssum_key):
            owb = sbt(st, "owb", [128, 32, D], BF16)
            ygin = [sbt(st, "ygin%d" % i, [128, DI], BF16) for i in range(2)]
            ygT = [sbt(st, "ygT%d" % i, [128, 32, 128], BF16) for i in range(2)]
            xin = [sbt(st, "xinC%d" % i, [128, D]) for i in range(2)]
            xnew = [sbt(st, "xnew%d" % i, [128, D]) for i in range(1)]
            gtb = sbt(st, "gtb", [128, 2 if li == 0 else 1, D])
            if li == 1:
                fwb = sbt(st, "fwb", [128, D])
            ss2 = sbt(st, "ss2", [128, NT]); ln2 = sbt(st, "ln2", [128, NT]); rs2 = sbt(st, "rs2", [128, NT])
            owv = ow_d.rearrange("(kc p) c -> p kc c", p=128)
            for q in range(8):
                K.dma("pool", owb[:, q * 4:(q + 1) * 4, :], owv[:, q * 4:(q + 1) * 4, :], w=["owb%d" % q])
            if li == 0:
                K.dma("sp", gtb[:, 0, :], gt_d[0].partition_broadcast(128), w=["gtb"])
                K.dma("sp", gtb[:, 1, :], gt_d[1].partition_broadcast(128), w=["gtb"])
            else:
                K.dma("sp", gtb[:, 0, :], gt_d[2].partition_broadcast(128), w=["gtb"])
                K.dma("sp", fwb[:], fnw_d.partition_broadcast(128), w=["fwb"])
            tiles = list(range(NT)) if li == 0 else list(range(2, NT))
            ntl = len(tiles)

            def c_load(it):
                t = tiles[it]
                yb = ygin[it % 2]; yk = "ygin%d" % (it % 2)
                if li == 0:
                    K.dma("sp", yb[:], yg_d[t * 128:(t + 1) * 128, :], w=[yk])
                else:
                    ygv = yg_d[CT:T, :].rearrange("(c r) ch -> r c ch", r=32)
                    r0 = 2 * (t - 2)
                    K.dma("sp", yb[0:64, :], ygv[r0], w=[yk])
                    K.dma("sp", yb[64:128, :], ygv[r0 + 1], w=[yk])
                src = csrc_d[t * 128:(t + 1) * 128, :] if t < 2 else xsrc_d[(t - 2) * 128:(t - 1) * 128, :]
                K.dma("sp", xin[it % 2][:], src, w=["xinC%d" % (it % 2)])

            def c_transp(it, q):
                yb = ygin[it % 2]; yk = "ygin%d" % (it % 2)
                yt = ygT[it % 2]; ytk = "ygT%d" % (it % 2)
                half = q % 2
                for j in range(4):
                    kc = q * 4 + j
                    K.tr(pbfs[half][:, j * 128:(j + 1) * 128], yb[:, kc * 128:(kc + 1) * 128], ident[:],
                         r=[yk, "ident"], w=["pb%d" % half], acc=(j > 0))
                for j in range(4):
                    kc = q * 4 + j
                    src_ap = pbfs[half][:, j * 128:(j + 1) * 128]
                    if j % 2 == 0:
                        K.act(yt[:, kc, :], src_ap, AF.Identity, scale=onT[:, li, kc:kc + 1], r=["pb%d" % half, "onT"], w=[ytk + "_%d" % kc])
                    else:
                        K.ts("dve", yt[:, kc, :], src_ap, onT[:, li, kc:kc + 1], None, ALU.mult, r=["pb%d" % half, "onT"], w=[ytk + "_%d" % kc])

            def c_mm(it, db):
                t = tiles[it]
                yt = ygT[it % 2]; ytk = "ygT%d" % (it % 2)
                xb = xin[it % 2]; xk = "xinC%d" % (it % 2)
                xw = xnew[0]; xwk = "xnew0"
                m = 1 if t < 2 else 0
                po = bank[db]; pk = "bk%d" % db
                for kc in range(32):
                    K.mm(po[:], yt[:, kc, :], owb[:, kc, db * 512:(db + 1) * 512], start=(kc == 0), stop=(kc == 31),
                         r=[ytk + "_%d" % kc, "owb%d" % (kc // 4)], w=[pk], acc=(kc > 0))
                K.stt("dve", xw[:, db * 512:(db + 1) * 512], po[:], rstdo[:, t:t + 1], gtb[:, m, db * 512:(db + 1) * 512],
                      ALU.mult, ALU.mult, r=[pk, ssum_key, "gtb"], w=[xwk + "_%d" % db])
                K.tt("dve", xw[:, db * 512:(db + 1) * 512], xw[:, db * 512:(db + 1) * 512], xb[:, db * 512:(db + 1) * 512], ALU.add,
                     r=[xwk + "_%d" % db, xk], w=[xwk + "_%d" % db])

            def c_fin(it):
                t = tiles[it]
                xw = xnew[0]; xwk = "xnew0"
                wk = [xwk + "_%d" % db for db in range(4)]
                if li == 0:
                    dst = ctx1_d[t * 128:(t + 1) * 128, :] if t < 2 else x1_d[(t - 2) * 128:(t - 1) * 128, :]
                    K.dma("sp", dst, xw[:], r=wk)
                else:
                    K.act(ygin[it % 2][:, 0:D], xw[:], AF.Square, accum=ss2[:, t:t + 1], r=wk, w=["ygin%d" % (it % 2), "ss2_%d" % t])
                    K.act(ln2[:, t:t + 1], ss2[:, t:t + 1], AF.Ln, bias=EPS, scale=1.0 / D, r=["ss2_%d" % t], w=["ln2_%d" % t])
                    K.act(rs2[:, t:t + 1], ln2[:, t:t + 1], AF.Exp, scale=-0.5, r=["ln2_%d" % t], w=["rs2_%d" % t])
                    K.stt("dve", xw[:], xw[:], rs2[:, t:t + 1], fwb[:], ALU.mult, ALU.mult, r=wk + ["rs2_%d" % t, "fwb"], w=wk)
                    K.dma("sp", out_d[(t - 2) * 128:(t - 1) * 128, :], xw[:], r=wk)

            c_load(0)
            c_load(1)
            for q in range(8):
                c_transp(0, q)
            for it in range(ntl):
                for db in range(4):
                    c_mm(it, db)
                    if it + 1 < ntl:
                        c_transp(it + 1, 2 * db)
                        c_transp(it + 1, 2 * db + 1)
                c_fin(it)
                if it + 2 < ntl:
                    c_load(it + 2)

        def finish_ss(st, ss_all, nparts, ndiv):
            sst = sbt(st, "sst", [128, NT])
            P.op("dve", lambda e: e.tensor_reduce(out=sst[:], in_=ss_all[:], axis=mybir.AxisListType.X, op=ALU.add),
                 ["ss_all"], ["sst"])
            K.act(sst[:], sst[:], AF.Ln, bias=EPS, scale=1.0 / ndiv, r=["sst"], w=["sst"])
            K.act(rstdo[:], sst[:], AF.Exp, scale=-0.5, r=["sst"], w=["rstdo"])

        with ExitStack() as stL:
            hT = sbt(stL, "hT", [128, KC, T], BF16)
            with ExitStack() as st:
                phaseA(st, 0, hT, x_d, ctx_d)
            P.barrier()
            if stop_after == "a0":
                dbg_d = nc.dram_tensor("dbg", [128, KC * T], BF16, kind="ExternalOutput").ap()
                K.dma("sp", dbg_d, hT[:].rearrange("p k t -> p (k t)"), r=[])
                P.emit(); return nc
            with ExitStack() as st:
                W = sbt(st, "W", [128, KC, 896], BF16)
                qd = [sbt(st, "qd%d" % d, [128, T], BF16) for d in range(2)]
                kd = [sbt(st, "kd%d" % d, [128, T], BF16) for d in range(2)]
                qs = [sbt(st, "qs%d" % d, [128, T], BF16) for d in range(2)]
                ketm = [sbt(st, "ketm%d" % d, [128, NT, 128], BF16) for d in range(2)]
                vtm = sbt(st, "vtm", [128, NT, 256], BF16)
                sg = sbt(st, "sg", [128, NT, 256], BF16)
                oacc = sbt(st, "oacc", [128, NT, 256], BF16)
                q16 = [sbt(st, "q16_%d" % i, [128, 512], BF16) for i in range(2)]
                tA = [sbt(st, "tA%d" % d, [128, 512]) for d in range(2)]
                tB = [sbt(st, "tB%d" % d, [128, 512]) for d in range(2)]
                tC = [sbt(st, "tC%d" % d, [128, 512]) for d in range(2)]
                k32 = [sbt(st, "k32%d" % d, [128, 512]) for d in range(2)]
                keT = [sbt(st, "keT%d" % d, [128, 512], BF16) for d in range(2)]
                lo4 = sbt(st, "lo", [128, 2, 2, 8]); hi4 = sbt(st, "hi", [128, 2, 2, 8]); mi4 = sbt(st, "mi", [128, 2, 2, 8])
                rq4 = sbt(st, "rq", [128, 2, 2, 8]); rk4 = sbt(st, "rk", [128, 2, 2, 8])
                aall = sbt(st, "aall", [128, 2, 36])
                S32 = [sbt(st, "S32_%d" % d, [128, 256]) for d in range(2)]
                Sbf = [sbt(st, "Sbf_%d" % d, [128, 6, 256], BF16) for d in range(2)]
                amask = [sbt(st, "amask%d" % d, [128, 64]) for d in range(2)]
                attm = [sbt(st, "attm%d" % d, [128, 2, 64], BF16) for d in range(2)]
                o32 = [sbt(st, "o32_%d" % i, [128, 256]) for i in range(2)]
                ygt = [sbt(st, "ygt%d" % i, [128, 256], BF16) for i in range(2)]
                junk = sbt(st, "junkB", [128, 256], BF16)
                sgt = [sbt(st, "sgt%d" % i, [128, 256]) for i in range(2)]
                g32 = [sbt(st, "g32_%d" % i, [128, 256]) for i in range(2)]
                ss_all = sbt(st, "ss_all", [128, NT, 16])
                K.memset("pool", ss_all[:], 0.0, w=["ss_all"])
                for d_ in range(2):
                    mk_ = maskF if d_ == 0 else maskB
                    K.cp("dve", amask[d_][0:64, :], mk_[0:64, 0:64], r=["maskF", "maskB"], w=["amask"])
                    K.cp("dve", amask[d_][64:128, :], mk_[64:128, 64:128], r=["maskF", "maskB", "amask"], w=["amask"])
                wv = a_in_w_d.rearrange("(kc p) c -> p kc c", p=128)

                def load_w(h):
                    K.dma("pool", W[:, :, 0:128], wv[:, :, h * 128:(h + 1) * 128], w=["Wq"])
                    K.dma("pool", W[:, :, 128:256], wv[:, :, 2048 + h * 128:2048 + (h + 1) * 128], w=["Wf0"])
                    K.dma("pool", W[:, :, 256:384], wv[:, :, 4096 + h * 128:4096 + (h + 1) * 128], w=["Wf1"])
                    K.dma("pool", W[:, :, 384:640], wv[:, :, 6144 + h * 256:6144 + (h + 1) * 256], w=["Wvg"])
                    K.dma("pool", W[:, :, 640:896], wv[:, :, 10240 + h * 256:10240 + (h + 1) * 256], w=["Wvg"])

                P.cap = int(_os.environ.get("B0CAP", "100000000")); P.cap_active = True
                load_w(0)
                fw_order = list(range(NT))
                bw_order = [1, 0] + list(range(NT - 1, 1, -1))
                yg_cnt = [0]
                for h in range(int(_os.environ.get("NHEADS", "16"))):
                    for (g, tok0, n) in GROUPS[0:int(_os.environ.get("NGROUPS", "5"))]:
                        nch = n // 64
                        hk = ["hT%d_%d" % (kc, g) for kc in range(KC)]
                        pq = bank[0]
                        for kc in range(KC):
                            K.mm(pq[:, 0:n], W[:, kc, 0:128], hT[:, kc, tok0:tok0 + n], start=(kc == 0), stop=(kc == KC - 1),
                                 r=["Wq", hk[kc]], w=["bk0"], acc=(kc > 0))
                        q32 = q16[g % 2]; q32k = "q16_%d" % (g % 2)
                        K.cp("act", q32[:, 0:n], pq[:, 0:n], r=["bk0"], w=[q32k])
                        def chain(d):
                            pf = bank[1 + d]; pfk = "bk%d" % (1 + d)
                            wk = "Wf%d" % d
                            for kc in range(KC):
                                K.mm(pf[:, 0:n], W[:, kc, 128 + d * 128:256 + d * 128], hT[:, kc, tok0:tok0 + n], start=(kc == 0), stop=(kc == KC - 1),
                                     r=[wk, hk[kc]], w=[pfk], acc=(kc > 0))
                                yield
                            a_, b_, c_ = tA[d], tB[d], tC[d]
                            ak, bk, ck, kk = "tA%d" % d, "tB%d" % d, "tC%d" % d, "k32%d" % d
                            lbc = lbT[:, d * 16 + h:d * 16 + h + 1]
                            e_eng = "dve"
                            K.act(a_[:, 0:n], pf[:, 0:n], AF.Exp, scale=-1.0, r=[pfk], w=[ak])
                            yield
                            K.act(b_[:, 0:n], a_[:, 0:n], AF.Ln, bias=1.0, scale=lbc, r=[ak, "lbT"], w=[bk])
                            yield
                            K.act(c_[:, 0:n], a_[:, 0:n], AF.Ln, bias=1.0, r=[ak], w=[ck])
                            yield
                            K.tt(e_eng, b_[:, 0:n], b_[:, 0:n], c_[:, 0:n], ALU.subtract, r=[bk, ck], w=[bk])
                            yield
                            K.act(c_[:, 0:n], b_[:, 0:n], AF.Exp, r=[bk], w=[ck])
                            yield
                            K.ts(e_eng, k32[d][:, 0:n], c_[:, 0:n], -1.0, 1.0, ALU.mult, ALU.add, r=[ck], w=[kk])
                            yield
                            K.scan(a_[:, 0:n], ones32[:, 0:1].to_broadcast([128, n]), b_[:, 0:n], r=["ones32", bk], w=[ak])
                            yield
                            a3 = a_[:, 0:n].rearrange("p (c t) -> p c t", t=64)
                            b3 = b_[:, 0:n].rearrange("p (c t) -> p c t", t=64)
                            sk_ = "sc%d_%d" % (d, g % 2)
                            lo = lo4[:, g % 2]; hi = hi4[:, g % 2]; mi = mi4[:, g % 2]; rq = rq4[:, g % 2]; rk = rk4[:, g % 2]
                            if d == 0:
                                K.tt("dve", lo[:, d, 0:nch], a3[:, :, 0], b3[:, :, 0], ALU.subtract, r=[ak, bk], w=[sk_ + "lo"])
                                yield
                                K.cp("dve", hi[:, d, 0:nch], a3[:, :, 63], r=[ak], w=[sk_ + "hi"])
                                yield
                            else:
                                K.cp("dve", hi[:, d, 0:nch], a3[:, :, 63], r=[ak], w=[sk_ + "hi"])
                                yield
                                K.tt("dve", a_[:, 0:n], a_[:, 0:n], b_[:, 0:n], ALU.subtract, r=[ak, bk], w=[ak])
                                yield
                                K.cp("dve", lo[:, d, 0:nch], a3[:, :, 0], r=[ak], w=[sk_ + "lo"])
                                yield
                            K.cp("dve", mi[:, d, 0:nch], a3[:, :, 31], r=[ak], w=[sk_ + "mi"])
                            yield
                            gc0 = (tok0 // 64)
                            K.tt("dve", rq[:, d, 0:nch], mi[:, d, 0:nch] if d == 0 else hi[:, d, 0:nch],
                                 lo[:, d, 0:nch] if d == 0 else mi[:, d, 0:nch], ALU.subtract,
                                 r=[sk_ + "lo", sk_ + "hi", sk_ + "mi"], w=[sk_ + "rq"])
                            yield
                            K.tt("dve", rk[:, d, 0:nch], hi[:, d, 0:nch] if d == 0 else mi[:, d, 0:nch],
                                 mi[:, d, 0:nch] if d == 0 else lo[:, d, 0:nch], ALU.subtract,
                                 r=[sk_ + "lo", sk_ + "hi", sk_ + "mi"], w=[sk_ + "rk"])
                            yield
                            K.tt("dve", aall[:, d, gc0:gc0 + nch], hi[:, d, 0:nch], lo[:, d, 0:nch], ALU.subtract,
                                 r=[sk_ + "lo", sk_ + "hi"], w=["aall%d_%d" % (d, g)])
                            yield
                            K.act(rq[:, d, 0:nch], rq[:, d, 0:nch], AF.Exp, r=[sk_ + "rq"], w=[sk_ + "rq"])
                            yield
                            K.act(rk[:, d, 0:nch], rk[:, d, 0:nch], AF.Exp, r=[sk_ + "rk"], w=[sk_ + "rk"])
                            yield
                            K.act(aall[:, d, gc0:gc0 + nch], aall[:, d, gc0:gc0 + nch], AF.Exp, r=["aall%d_%d" % (d, g)], w=["aall%d_%d" % (d, g)])
                            yield
                            K.tt(e_eng, a3, a3, mi[:, d, 0:nch].unsqueeze(2).to_broadcast([128, nch, 64]), ALU.subtract,
                                 r=[ak, sk_ + "mi"], w=[ak])
                            yield
                            K.act(b_[:, 0:n], a_[:, 0:n], AF.Exp, r=[ak], w=[bk])
                            yield
                            K.act(c_[:, 0:n], a_[:, 0:n], AF.Exp, scale=-1.0, r=[ak], w=[ck])
                            yield
                            qf, kf = (b_, c_) if d == 0 else (c_, b_)
                            qfk, kfk = (bk, ck) if d == 0 else (ck, bk)
                            qdk, kdk, qsk = "qd%d_%d" % (d, g), "kd%d_%d" % (d, g), "qs%d_%d" % (d, g)
                            K.tt("dve", qd[d][:, tok0:tok0 + n], q32[:, 0:n], qf[:, 0:n], ALU.mult, r=[q32k, qfk], w=[qdk])
                            yield
                            K.tt("pool", kd[d][:, tok0:tok0 + n], k32[d][:, 0:n], kf[:, 0:n], ALU.mult, r=[kk, kfk], w=[kdk])
                            yield
                            K.tt("dve", qs[d][:, tok0:tok0 + n].rearrange("p (c t) -> p c t", t=64),
                                 qd[d][:, tok0:tok0 + n].rearrange("p (c t) -> p c t", t=64),
                                 rq[:, d, 0:nch].unsqueeze(2).to_broadcast([128, nch, 64]), ALU.mult, r=[qdk, sk_ + "rq"], w=[qsk])
                            yield
                            K.tt("pool", keT[d][:, 0:n].rearrange("p (c t) -> p c t", t=64),
                                 kd[d][:, tok0:tok0 + n].rearrange("p (c t) -> p c t", t=64),
                                 rk[:, d, 0:nch].unsqueeze(2).to_broadcast([128, nch, 64]), ALU.mult, r=[kdk, sk_ + "rk"], w=["keT%d" % d])
                            yield
                            for tl in range(n // 128):
                                t = tok0 // 128 + tl
                                half = d
                                K.tr(pbfs[half][:, tl * 128:(tl + 1) * 128], keT[d][:, tl * 128:(tl + 1) * 128], ident[:],
                                     r=["keT%d" % d, "ident"], w=["pb%d" % half], acc=(tl > 0))
                                yield
                            t0 = tok0 // 128
                            K.cp("act", ketm[d][:, t0:t0 + n // 128, :], pbfs[d][:, 0:n].rearrange("p (t k) -> p t k", k=128),
                                 r=["pb%d" % d], w=["ketm%d_%d" % (d, g)])
                            yield
                        def vg():
                            for tl in range(n // 128):
                                t = tok0 // 128 + tl
                                bi_ = 3 + (t % 3)
                                pt = bank[bi_]; ptk = "bk%d" % bi_
                                for kc in range(KC):
                                    K.mm(pt[:], hT[:, kc, t * 128:(t + 1) * 128], W[:, kc, 384:896], start=(kc == 0), stop=(kc == KC - 1),
                                         r=["Wvg", hk[kc]], w=[ptk], acc=(kc > 0))
                                    yield
                                i_ = t % 2
                                K.cp("dve", vtm[:, t, :], pt[:, 0:256], r=[ptk], w=["vtm%d" % t])
                                yield
                                K.cp("act", g32[i_][:], pt[:, 256:512], r=[ptk], w=["g32_%d" % i_])
                                yield
                                K.act(sgt[i_][:], g32[i_][:], AF.Exp, scale=-1.0, r=["g32_%d" % i_], w=["sgt%d" % i_])
                                yield
                                K.act(sgt[i_][:], sgt[i_][:], AF.Ln, bias=1.0, r=["sgt%d" % i_], w=["sgt%d" % i_])
                                yield
                                K.act(sgt[i_][:], sgt[i_][:], AF.Exp, scale=-1.0, r=["sgt%d" % i_], w=["sgt%d" % i_])
                                yield
                                K.tt("pool", sg[:, t, :], g32[i_][:], sgt[i_][:], ALU.mult, r=["g32_%d" % i_, "sgt%d" % i_], w=["sg%d" % t])
                                yield
                        gens = [chain(0), chain(1), vg()]
                        if _os.environ.get("NOILV"):
                            for g_ in gens:
                                for _ in g_:
                                    pass
                            gens = []
                        while gens:
                            for g_ in list(gens):
                                try:
                                    next(g_)
                                except StopIteration:
                                    gens.remove(g_)
                    if _os.environ.get("B0STOP") == "proj":
                        break
                    if h + 1 < 16:
                        load_w(h + 1)
                    RS = 6
                    for d in range(2):
                        K.memset("pool", S32[d][:], 0.0, w=["S32_%d" % d])
                        K.memset("pool", Sbf[d][:, 0, :], 0.0, w=["Sbf%d_0" % d])
                    visited = set()

                    def step_info(step):
                        info = []
                        for d in range(2):
                            t = fw_order[step] if d == 0 else bw_order[step]
                            chunks = (2 * t, 2 * t + 1) if d == 0 else (2 * t + 1, 2 * t)
                            info.append((d, t, tile_group(t), chunks))
                        return info

                    def scan_front(step):
                        info = step_info(step)
                        ab = step % 2
                        for (d, t, g, chunks) in info:
                            for c in chunks:
                                hp = c % 2; rows = slice(hp * 64, hp * 64 + 64); csl = slice(c * 64, c * 64 + 64)
                                K.mm(bank[d][rows, 0:64], kd[d][:, csl], qd[d][:, csl],
                                     r=["kd%d_%d" % (d, g), "qd%d_%d" % (d, g)], w=["bk%d" % d], acc=(c != chunks[0]))
                        for (d, t, g, chunks) in info:
                            K.tt("dve", attm[d][:, ab, :], bank[d][:, 0:64], amask[d][:, :], ALU.mult,
                                 r=["bk%d" % d, "amask"], w=["attm%d_%d" % (d, ab)])
                        for (d, t, g, chunks) in info:
                            for c in chunks:
                                hp = c % 2; rows = slice(hp * 64, hp * 64 + 64)
                                K.mm(bank[4 + d][:, hp * 256:(hp + 1) * 256], ketm[d][rows, t, :], vtm[rows, t, :],
                                     r=["ketm%d_%d" % (d, g), "vtm%d" % t], w=["bk%d" % (4 + d)])
                        for (d, t, g, chunks) in info:
                            for ci, c in enumerate(chunks):
                                hp = c % 2
                                slot = (2 * step + ci + 1) % RS
                                K.stt("dve", S32[d][:], S32[d][:], aall[:, d, c:c + 1], bank[4 + d][:, hp * 256:(hp + 1) * 256], ALU.mult, ALU.add,
                                      r=["S32_%d" % d, "aall%d_%d" % (d, g), "bk%d" % (4 + d)], w=["S32_%d" % d])
                                K.cp("act", Sbf[d][:, slot, :], S32[d][:], r=["S32_%d" % d], w=["Sbf%d_%d" % (d, slot)])

                    def scan_back(step):
                        nonlocal_cnt = None
                        info = step_info(step)
                        ab = step % 2
                        for (d, t, g, chunks) in info:
                            po = bank[2 + d][:, 0:256]; pok = "bk%d" % (2 + d)
                            for ci, c in enumerate(chunks):
                                hp = c % 2; rows = slice(hp * 64, hp * 64 + 64); csl = slice(c * 64, c * 64 + 64)
                                slot = (2 * step + ci) % RS
                                K.mm(po[rows, :], attm[d][rows, ab, :], vtm[rows, t, :], start=True, stop=False,
                                     r=["attm%d_%d" % (d, ab), "vtm%d" % t], w=[pok], acc=(ci > 0))
                                K.mm(po[rows, :], qs[d][:, csl], Sbf[d][:, slot, :], start=False, stop=True,
                                     r=["qs%d_%d" % (d, g), "Sbf%d_%d" % (d, slot)], w=[pok], acc=True)
                            if t not in visited:
                                visited.add(t)
                                K.cp("act", oacc[:, t, :], po, r=[pok], w=["oacc%d" % t])
                            else:
                                i2 = yg_cnt[0] % 2; yg_cnt[0] += 1
                                K.tt("dve", o32[i2][:], po, oacc[:, t, :], ALU.add,
                                     r=[pok, "oacc%d" % t], w=["o32_%d" % i2])
                                K.act(junk[:], o32[i2][:], AF.Square, accum=ss_all[:, t, h:h + 1], r=["o32_%d" % i2], w=["junkB", "ss_all"])
                                K.tt("pool", ygt[i2][:], o32[i2][:], sg[:, t, :], ALU.mult, r=["o32_%d" % i2, "sg%d" % t], w=["ygt%d" % i2])
                                K.dma("sp", yg_d[t * 128:(t + 1) * 128, h * 256:(h + 1) * 256], ygt[i2][:], r=["ygt%d" % i2])

                    scan_front(0)
                    for step in range(NT):
                        if step + 1 < NT:
                            scan_front(step + 1)
                        scan_back(step)
                P.cap_active = False
                finish_ss(st, ss_all, NT, float(DI))
            P.barrier()
        if stop_after == "b0":
            P.emit(); return nc
        with ExitStack() as st:
            phaseC(st, 0, a_out_w_d, x_d, ctx_d, "rstdo")
        P.barrier()
        if stop_after == "c0":
            P.emit(); return nc

        with ExitStack() as stL:
            hT = sbt(stL, "hT1", [128, KC, T], BF16)
            with ExitStack() as st:
                phaseA(st, 1, hT, x1_d, ctx1_d)
            P.barrier()
            if stop_after == "a1":
                dbg_d = nc.dram_tensor("dbg", [128, KC * T], BF16, kind="ExternalOutput").ap()
                K.dma("sp", dbg_d, hT[:].rearrange("p k t -> p (k t)"), r=[])
                P.emit(); return nc
            with ExitStack() as st:
                wv = b_in_w_d.rearrange("(kc p) c -> p kc c", p=128)
                Wr = sbt(st, "Wr", [128, KC, 512], BF16)
                biasTM = sbt(st, "biasTM", [128, NT, 128])
                sOutTM = sbt(st, "sOutTM", [128, NT, 128], BF16)
                wSTM = sbt(st, "wSTM", [128, NT, 128], BF16)
                abc = sbt(st, "abc", [128, NT, 128])
                QQ = [sbt(st, "QQ%d" % d, [128, T], BF16) for d in range(2)]
                id2 = sbt(st, "id2", [128, 64], BF16)
                onesb = sbt(st, "onesb1", [128, 128])
                ncbT = sbt(st, "ncbT", [128, 48])
                K.memset("dve", onesb[:], 1.0, w=["onesb"])
                K.ts("dve", ncbT[:], cbT[:], -1.0, None, ALU.mult, r=["cbT"], w=["ncbT"])
                CH = [(c, c * 128) for c in range(NT)]
                with ExitStack() as sp_:
                    DT = sbt(sp_, "DT", [128, T]); lnD = sbt(sp_, "lnD", [128, T]); dtA = sbt(sp_, "dtA", [128, T])
                    PP = sbt(sp_, "PP", [128, T]); YY = sbt(sp_, "YY", [128, T]); bT = sbt(sp_, "bT", [128, T])
                    sOT = lnD; wST = DT; tmpP = dtA
                    Ptot = sbt(sp_, "Ptot", [128, NT]); aC = sbt(sp_, "aC", [128, NT]); Z = sbt(sp_, "Z", [128, NT, 128])
                    Qhi = sbt(sp_, "Qhi", [128, T], BF16); Qlo = sbt(sp_, "Qlo", [128, T], BF16)
                    K.dma("pool", Wr[:, :, 0:128], wv[:, :, 10240:10368], w=["Wr0"])
                    for (g, tok0, n) in GROUPS:
                        pf = bank[g % 2][:, 0:n]; pfk = "bk%d" % (g % 2)
                        for kc in range(KC):
                            K.mm(pf, Wr[:, kc, 0:128], hT[:, kc, tok0:tok0 + n], start=(kc == 0), stop=(kc == KC - 1),
                                 r=["Wr0"], w=[pfk], acc=(kc > 0))
                        K.act(DT[:, tok0:tok0 + n], pf, AF.Exp, bias=dtb[:, 0:1], r=[pfk, "dtb"], w=["DT%d" % g])
                        K.act(DT[:, tok0:tok0 + n], DT[:, tok0:tok0 + n], AF.Ln, bias=1.0, r=["DT%d" % g], w=["DT%d" % g])
                    dk = ["DT%d" % g for g in range(5)]
                    K.act(lnD[:], DT[:], AF.Ln, r=dk, w=["lnD"])
                    K.ts("pool", dtA[:], DT[:], aneg[:, 0:1], None, ALU.mult, r=dk + ["aneg"], w=["dtA"])
                    for (c, t0) in CH:
                        K.scan(PP[:, t0:t0 + 128], onesb[:, 0:128], dtA[:, t0:t0 + 128], r=["onesb", "dtA"], w=["PP%d" % c])
                    pk_ = ["PP%d" % c for c in range(NT)]
                    K.cp("pool", YY[:], PP[:], r=pk_, w=["YY"])
                    K.tt("pool", YY[64:128, :], PP[64:128, :], dtA[64:128, :], ALU.subtract, r=pk_ + ["dtA", "YY"], w=["YY"])
                    K.cp("pool", Ptot[:], PP[:].rearrange("p (c t) -> p c t", t=128)[:, :, 127], r=pk_, w=["Ptot"])
                    K.tt("dve", bT[0:64, :], lnD[0:64, :], PP[0:64, :], ALU.subtract, r=["lnD"] + pk_, w=["bT"])
                    K.tt("dve", bT[64:128, :], lnD[64:128, :], YY[64:128, :], ALU.add, r=["lnD", "YY", "bT"], w=["bT"])
                    ptb = Ptot[:].unsqueeze(2).to_broadcast([128, NT, 128])
                    K.act(sOT[0:64, :], PP[0:64, :], AF.Exp, r=pk_, w=["lnD"])
                    K.tt("pool", tmpP[:].rearrange("p (c t) -> p c t", t=128), YY[:].rearrange("p (c t) -> p c t", t=128), ptb,
                         ALU.subtract, r=["YY", "Ptot"], w=["dtA"])
                    K.act(sOT[64:128, :], tmpP[64:128, :], AF.Exp, scale=-1.0, r=["dtA", "lnD"], w=["lnD"])
                    K.tt("pool", tmpP[:].rearrange("p (c t) -> p c t", t=128), bT[:].rearrange("p (c t) -> p c t", t=128), ptb,
                         ALU.add, r=["bT", "Ptot", "dtA"], w=["dtA"])
                    K.act(wST[0:64, :], tmpP[0:64, :], AF.Exp, r=["dtA"], w=dk)
                    K.act(wST[64:128, :], bT[64:128, :], AF.Exp, r=["bT"] + dk, w=dk)
                    K.act(aC[:], Ptot[:], AF.Exp, r=["Ptot"], w=["aC"])
                    for (c, t0) in CH:
                        K.ts("pool" if c % 2 else "dve", Z[:, c, :], ident32[:], aC[:, c:c + 1], None, ALU.mult, r=["ident32", "aC"], w=["Z%d" % c])
                    Z2 = Z[:].rearrange("p c r -> p (c r)"); abc2 = abc[:].rearrange("p c r -> p (c r)")
                    for q in range(5):
                        w_ = min(512, NT * 128 - q * 512)
                        pz = bank[2 + q % 2][:, 0:w_]; pzk = "bk%d" % (2 + q % 2)
                        K.mm(pz, ones32[:], Z2[:, q * 512:q * 512 + w_], r=["ones32"] + ["Z%d" % c for c in range(NT)], w=[pzk])
                        K.cp("act", abc2[:, q * 512:q * 512 + w_], pz, r=[pzk], w=["abc"])
                    K.cp("dve", Qhi[:], YY[:], r=["YY"], w=["Qhi"])
                    K.tt("dve", Qlo[:], YY[:], Qhi[:], ALU.subtract, r=["YY", "Qhi"], w=["Qlo"])
                    K.tt("dve", id2[:], ident[:, 0:64], ident[:, 64:128], ALU.add, r=["ident"], w=["id2"])
                    K.cp("pool", QQ[0][0:64, :], Qhi[0:64, :], r=["Qhi"], w=["QQ0a"])
                    K.cp("pool", QQ[1][64:128, :], Qlo[64:128, :], r=["Qlo"], w=["QQ1b"])
                    K.dma("sp", QQ[0][64:128, :], Qlo[0:64, :], r=["Qlo"], w=["QQ0b"])
                    K.dma("sp", QQ[1][0:64, :], Qhi[64:128, :], r=["Qhi"], w=["QQ1a"])
                    for (c, t0) in CH:
                        for ai, (src, dst, dk_) in enumerate(((bT, biasTM, "biasTM"), (sOT, sOutTM, "sOutTM"), (wST, wSTM, "wSTM"))):
                            pb_ = bank[4 + (ai + c) % 2][:, 0:128]; pbk = "bk%d" % (4 + (ai + c) % 2)
                            K.tr(pb_, src[:, t0:t0 + 128], ident32[:], r=["bT", "lnD", "ident32"] + dk, w=[pbk])
                            K.cp("act" if (ai + c) % 2 else "dve", dst[:, c, :], pb_, r=[pbk], w=[dk_])
                P.barrier()
                B_tm = sbt(st, "B_tm", [128, NT, 128], BF16); BT = sbt(st, "BT", [128, T], BF16); CTt = sbt(st, "CTt", [128, T], BF16)
                x_tm = sbt(st, "x_tm", [128, NT, 256], BF16); sz = sbt(st, "sz", [128, 16, 256], BF16)
                Yint = sbt(st, "Yint", [128, 16, 256], BF16)
                rawT = sbt(st, "rawT", [128, T + 8], BF16); xcT = sbt(st, "xcT", [128, T], BF16)
                diag = sbt(st, "diag", [128, 5, 128], BF16)
                cbS = [sbt(st, "cbS%d" % d, [128, 128]) for d in range(2)]
                Et = [sbt(st, "Et%d" % d, [128, 128]) for d in range(3)]
                Wt = [sbt(st, "Wt%d" % d, [128, 128], BF16) for d in range(3)]
                H32 = [sbt(st, "H32_%d" % d, [128, 256]) for d in range(2)]
                Hbf = [sbt(st, "Hbf_%d" % d, [128, 256], BF16) for d in range(2)]
                xs_ = [sbt(st, "xs%d" % d, [128, 256], BF16) for d in range(2)]
                tmp1 = [sbt(st, "tmp1_%d" % d, [128, 256]) for d in range(2)]
                sgt = [sbt(st, "sgt1_%d" % d, [128, 512]) for d in range(2)]
                u32 = [sbt(st, "u32_%d" % d, [128, 512]) for d in range(2)]
                t1 = [sbt(st, "t1_%d" % d, [128, 256]) for d in range(2)]
                ygt = [sbt(st, "ygt1_%d" % d, [128, 256], BF16) for d in range(2)]
                junk = sbt(st, "junkB1", [128, 256], BF16)
                ss_all = sbt(st, "ss_all1", [128, NT, 16])
                K.memset("pool", ss_all[:], 1.0, w=["ss_all"])
                K.memset("pool", rawT[:], 0.0, w=["rawT"])

                def rawcol(tok0):
                    return tok0 + 2 if tok0 < CT else tok0 + 6

                sil_i = [0]

                def proj_conv(slot, col0, cwidx, dest, dkey, need_ctx):
                    wk = "Wr%d" % slot
                    K.dma("pool", Wr[:, :, slot * 128:(slot + 1) * 128], wv[:, :, col0:col0 + 128], w=[wk])
                    grp = GROUPS if need_ctx else GROUPS[1:]
                    for (g, tok0, n) in grp:
                        pf = bank[g % 2][:, 0:n]; pfk = "bk%d" % (g % 2)
                        for kc in range(KC):
                            K.mm(pf, Wr[:, kc, slot * 128:(slot + 1) * 128], hT[:, kc, tok0:tok0 + n], start=(kc == 0), stop=(kc == KC - 1),
                                 r=[wk], w=[pfk], acc=(kc > 0))
                        rc0 = rawcol(tok0)
                        K.cp("act", rawT[:, rc0:rc0 + n], pf, r=[pfk], w=["raw%d" % g])
                    for j in range(5):
                        K.act(diag[:, j, :], ident[:], AF.Identity, scale=cwT[:, cwidx, j:j + 1], r=["ident", "cwT"], w=["diag"])
                    for (g, tok0, n) in grp:
                        pc = bank[2 + g % 2][:, 0:n]; pck = "bk%d" % (2 + g % 2)
                        rc0 = rawcol(tok0)
                        rk_ = ["raw%d" % gg for gg in range(5)]
                        for j in range(5):
                            K.mm(pc, diag[:, j, :], rawT[:, rc0 + j - 2:rc0 + j - 2 + n], start=(j == 0), stop=(j == 4),
                                 r=["diag"] + rk_, w=[pck], acc=(j > 0))
                        i_ = sil_i[0] % 2; sil_i[0] += 1
                        e_ = sgt[i_][:, 0:n]; ek = "sgt%d" % i_
                        u_ = u32[i_][:, 0:n]; uk = "u32_%d" % i_
                        K.act(u_, pc, AF.Identity, bias=cbT[:, cwidx:cwidx + 1], r=[pck, "cbT"], w=[uk])
                        K.act(e_, u_, AF.Exp, scale=-1.0, r=[uk], w=[ek])
                        K.act(e_, e_, AF.Ln, bias=1.0, r=[ek], w=[ek])
                        K.act(e_, e_, AF.Exp, scale=-1.0, r=[ek], w=[ek])
                        K.tt("dve" if g % 2 else "pool", dest[:, tok0:tok0 + n], u_, e_, ALU.mult, r=[uk, ek], w=[dkey + "_%d" % g])

                def to_tm(src, skey, dst3, c0col, dkey, need_ctx):
                    for t in (range(NT) if need_ctx else range(2, NT)):
                        g = tile_group(t)
                        pb_ = pbfs[t % 2][:, 0:128]; pbk = "pb%d" % (t % 2)
                        K.tr(pb_, src[:, t * 128:(t + 1) * 128], ident[:], r=[skey + "_%d" % g, "ident"], w=[pbk])
                        K.cp("act" if t % 2 else "dve", dst3[:, t, c0col:c0col + 128], pb_, r=[pbk], w=[dkey + "_%d" % t])

                fwc = list(range(NT)); bwc = [1, 0] + list(range(NT - 1, 1, -1))
                for g in range(int(_os.environ.get("NGRP1", "8"))):
                    proj_conv(0, 8192 + g * 128, 32 + g, BT, "BT", True)
                    to_tm(BT, "BT", B_tm, 0, "B_tm", True)
                    proj_conv(1, 9216 + g * 128, 40 + g, CTt, "CT", False)
                    for half in range(2):
                        u = 2 * g + half
                        hd0 = g * 8 + half * 4
                        for b in range(2):
                            xb = g * 4 + half * 2 + b
                            proj_conv(b, 4096 + xb * 128, xb, xcT, "xcT", True)
                            to_tm(xcT, "xcT", x_tm, b * 128, "x_tm%d" % b, True)
                        xk_all = ["x_tm%d_%d" % (b, t) for b in range(2) for t in range(NT)]
                        for b in range(2):
                            zc = (g * 4 + half * 2 + b) * 128
                            K.dma("pool", Wr[:, :, 256 + b * 128:256 + (b + 1) * 128], wv[:, :, zc:zc + 128], w=["Wr%d" % (2 + b)])
                        for t in range(2, NT):
                            pt = bank[2 + t % 4][:, 0:256]; ptk = "bk%d" % (2 + t % 4)
                            for kc in range(KC):
                                K.mm(pt, hT[:, kc, t * 128:(t + 1) * 128], Wr[:, kc, 256:512], start=(kc == 0), stop=(kc == KC - 1),
                                     r=["Wr2", "Wr3"], w=[ptk], acc=(kc > 0))
                            i_ = sil_i[0] % 2; sil_i[0] += 1
                            e_ = sgt[i_][:, 0:256]; ek = "sgt%d" % i_
                            u_ = u32[i_][:, 0:256]; uk = "u32_%d" % i_
                            K.cp("act", u_, pt, r=[ptk], w=[uk])
                            K.act(e_, u_, AF.Exp, scale=-1.0, r=[uk], w=[ek])
                            K.act(e_, e_, AF.Ln, bias=1.0, r=[ek], w=[ek])
                            K.act(e_, e_, AF.Exp, scale=-1.0, r=[ek], w=[ek])
                            K.tt("dve" if t % 2 else "pool", sz[:, t - 2, :], u_, e_, ALU.mult, r=[uk, ek], w=["sz%d" % t])
                        for d in range(2):
                            K.memset("pool", H32[d][:], 0.0, w=["H32_%d" % d])
                            K.memset("pool", Hbf[d][:], 0.0, w=["Hbf_%d" % d])
                        visited = set()
                        for step in range(NT):
                            for d in range(2):
                                c = fwc[step] if d == 0 else bwc[step]
                                r0 = d * 64 + hd0
                                xk = ["x_tm%d_%d" % (b, c) for b in range(2)]
                                if c >= 2:
                                    yi = bank[0 + d][:, 0:256]; yik = "bk%d" % d
                                    K.mm(yi, CTt[:, c * 128:(c + 1) * 128], Hbf[d][:], r=["CT_%d" % tile_group(c), "Hbf_%d" % d], w=[yik])
                                    scb = sOutTM[:, c, r0:r0 + 4].unsqueeze(2).to_broadcast([128, 4, 64])
                                    yi3 = yi.rearrange("p (h e) -> p h e", e=64)
                                    if c not in visited:
                                        visited.add(c)
                                        K.tt("dve", Yint[:, c - 2, :].rearrange("p (h e) -> p h e", e=64), yi3, scb, ALU.mult,
                                             r=[yik, "sOutTM"], w=["Yint%d" % c])
                                    else:
                                        K.tt("dve", tmp1[d][:].rearrange("p (h e) -> p h e", e=64), yi3, scb, ALU.mult,
                                             r=[yik, "sOutTM"], w=["tmp1_%d" % d])
                                        K.tt("dve", Yint[:, c - 2, :], Yint[:, c - 2, :], tmp1[d][:], ALU.add,
                                             r=["Yint%d" % c, "tmp1_%d" % d], w=["Yint%d" % c])
                                if step < NT - 1:
                                    wsb = wSTM[:, c, r0:r0 + 4].unsqueeze(2).to_broadcast([128, 4, 64])
                                    K.tt("pool", xs_[d][:].rearrange("p (h e) -> p h e", e=64), x_tm[:, c, :].rearrange("p (h e) -> p h e", e=64),
                                         wsb, ALU.mult, r=xk + ["wSTM"], w=["xs%d" % d])
                                    pu = bank[2 + d][:, 0:256]; puk = "bk%d" % (2 + d)
                                    K.mm(pu, B_tm[:, c, :], xs_[d][:], r=["B_tm_%d" % c, "xs%d" % d], w=[puk])
                                    acb = abc[:, c, r0:r0 + 4].unsqueeze(2).to_broadcast([128, 4, 64])
                                    h3 = H32[d][:].rearrange("p (h e) -> p h e", e=64)
                                    K.tt("dve", h3, h3, acb, ALU.mult, r=["H32_%d" % d, "abc"], w=["H32_%d" % d])
                                    K.tt("dve", H32[d][:], H32[d][:], pu, ALU.add, r=["H32_%d" % d, puk], w=["H32_%d" % d])
                                    K.cp("act", Hbf[d][:], H32[d][:], r=["H32_%d" % d], w=["Hbf_%d" % d])
                        its = [(c, j, d) for c in range(2, NT) for j in range(4) for d in range(2)]
                        NI = len(its)

                        def p2_prologue(c):
                            cs = slice(c * 128, (c + 1) * 128)
                            pcb = bank[2 + c % 2][:, 0:128]; pck = "bk%d" % (2 + c % 2)
                            K.mm(pcb, BT[:, cs], CTt[:, cs], r=["BT_%d" % tile_group(c), "CT_%d" % tile_group(c)], w=[pck])
                            K.cp("act", cbS[c % 2][:], pcb, r=[pck], w=["cbS%d" % (c % 2)])

                        def p2_sel(i):
                            c, j, d = its[i]
                            cs = slice(c * 128, (c + 1) * 128)
                            r = d * 64 + hd0 + j
                            dps = bank[d][:, 0:128]; dpk = "bk%d" % d
                            jr = hd0 + j
                            sel = id2[:, jr:jr + 1].to_broadcast([128, 128])
                            K.mm(dps, sel, QQ[d][:, cs], start=True, stop=False, r=["id2"], w=[dpk])
                            K.mm(dps, ident[:], (negF if d == 0 else negB)[:], start=False, stop=True, r=["ident", "negF", "negB"], w=[dpk], acc=True)

                        def p2_exp(i):
                            c, j, d = its[i]
                            r = d * 64 + hd0 + j
                            k3 = i % 3
                            K.act(Et[k3][:], bank[d][:, 0:128], AF.Exp, bias=biasTM[:, c, r:r + 1], scale=(1.0 if d == 0 else -1.0),
                                  r=["bk%d" % d, "biasTM"], w=["Et%d" % k3])
                            K.tt("dve", Wt[k3][:], Et[k3][:], cbS[c % 2][:], ALU.mult, r=["Et%d" % k3, "cbS%d" % (c % 2)], w=["Wt%d" % k3])

                        def p2_y(i):
                            c, j, d = its[i]
                            k3 = i % 3
                            yps = bank[4 + c % 2][:, 0:256]
                            xk = ["x_tm%d_%d" % (b, c) for b in range(2)]
                            K.mm(yps[:, j * 64:(j + 1) * 64], Wt[k3][:], x_tm[:, c, j * 64:(j + 1) * 64], start=(d == 0), stop=(d == 1),
                                 r=["Wt%d" % k3] + xk, w=["bk%d" % (4 + c % 2)], acc=(not (j == 0 and d == 0)))

                        def p2_final(c):
                            i2 = c % 2
                            yps = bank[4 + c % 2][:, 0:256]
                            xk = ["x_tm%d_%d" % (b, c) for b in range(2)]
                            t3 = t1[i2][:].rearrange("p (h e) -> p h e", e=64)
                            K.tt("pool", t3, x_tm[:, c, :].rearrange("p (h e) -> p h e", e=64),
                                 Dbc[:, hd0:hd0 + 4].unsqueeze(2).to_broadcast([128, 4, 64]), ALU.mult, r=xk + ["Dbc"], w=["t1_%d" % i2])
                            K.tt("pool", t1[i2][:], t1[i2][:], Yint[:, c - 2, :], ALU.add, r=["t1_%d" % i2, "Yint%d" % c], w=["t1_%d" % i2])
                            K.tt("dve", t1[i2][:], t1[i2][:], yps, ALU.add, r=["t1_%d" % i2, "bk%d" % (4 + c % 2)], w=["t1_%d" % i2])
                            K.tt("pool", ygt[i2][:], t1[i2][:], sz[:, c - 2, :], ALU.mult, r=["t1_%d" % i2, "sz%d" % c], w=["ygt%d" % i2])
                            K.act(junk[:], ygt[i2][:], AF.Square, accum=ss_all[:, c, u:u + 1], r=["ygt%d" % i2], w=["junkB1", "ss_all"])
                            K.dma("sp", yg_d[c * 128:(c + 1) * 128, u * 256:(u + 1) * 256], ygt[i2][:], r=["ygt%d" % i2])

                        for i in range(NI + 2):
                            if i < NI:
                                c, j, d = its[i]
                                if j == 0 and d == 0:
                                    p2_prologue(c)
                                p2_sel(i)
                            if 1 <= i <= NI:
                                p2_exp(i - 1)
                            if i >= 2:
                                p2_y(i - 2)
                                c2, j2, d2 = its[i - 2]
                                if j2 == 3 and d2 == 1:
                                    p2_final(c2)
                finish_ss(st, ss_all, NT, float(DI))
                K.dma("sp", rs_d.rearrange("(c p) -> p c", p=128), rstdo[:, 2:NT], r=["rstdo"], w=["rs_d"], slow=True)
                for hb in range(2):
                    K.dma("sp", rstdo[hb * 64:(hb + 1) * 64, 2:NT], rs_d.rearrange("(col t two) -> two col t", two=2, t=16)[hb],
                          r=["rs_d"], w=["rstdo"], slow=True)
            P.barrier()
        if stop_after == "b1":
            P.emit(); return nc
        with ExitStack() as st:
            phaseC(st, 1, b_out_w_d, x1_d, ctx1_d, "rstdo")
        P.barrier()

        P.emit()
    return nc


_CACHE = {}


def kernel(x, c, ctx, c_ctx, norm_w, mod_w, mod_b, a_in_w, a_lb, a_onorm_w, a_out_w,
           b_in_w, b_conv_w, b_conv_b, b_dt_bias, b_a_log, b_d, b_onorm_w, b_out_w, final_norm_w):
    f = lambda a: np.ascontiguousarray(np.asarray(a, dtype=np.float32))
    shared = {
        "c_ctx": f(c_ctx), "norm_w": f(norm_w), "mod_w": f(mod_w), "mod_b": f(mod_b),
        "a_in_w": f(a_in_w)[0], "a_lb": f(a_lb), "a_onorm_w": f(a_onorm_w)[0], "a_out_w": f(a_out_w)[0],
        "b_in_w": f(b_in_w)[0], "b_conv_w": f(b_conv_w)[0], "b_conv_b": f(b_conv_b)[0],
        "b_dt_bias": f(b_dt_bias).reshape(128), "b_a_log": f(b_a_log).reshape(128), "b_d": f(b_d)[0],
        "b_onorm_w": f(b_onorm_w)[0], "b_out_w": f(b_out_w)[0], "final_norm_w": f(final_norm_w),
    }
    x = f(x); c = f(c); ctx = f(ctx)
    if "nc" not in _CACHE:
        _CACHE["nc"] = build_nc()
    nc = _CACHE["nc"]
    in_maps = []
    for b in range(8):
        m = dict(shared)
        m["x"] = x[b]; m["c"] = c[b]; m["ctx"] = ctx[b]
        in_maps.append(m)
    res = run_bass_kernel_spmd(nc, in_maps, core_ids=list(range(8)))
    return np.stack([np.asarray(r["out"], dtype=np.float32) for r in res.results], axis=0)
```
